# Optimizing a Trainium2 kernel written in Bass

```python
import math
import jax, jax.numpy as jnp
from jax import lax
import numpy as np

D_MODEL = 2048
BATCH = 8
SEQ = 2048
DEPTH = 1
DEC_BATCH = 128
DEC_SEQ = 1
PAST_LEN = 16384
PAGE_SIZE = 128

N_META = 16
D_SSM = D_MODEL // 2
SSM_GROUP = 16
N_SSM_GROUPS = D_SSM // SSM_GROUP
SSM_STATE = 64
HEAD_DIM = 64
N_HEADS = (D_MODEL - D_SSM) // HEAD_DIM
N_KV_HEADS = 4
Q_PER_KV = N_HEADS // N_KV_HEADS
D_ATTN = N_HEADS * HEAD_DIM
D_KV = N_KV_HEADS * HEAD_DIM
D_IN = D_SSM + D_ATTN + 2 * D_KV
WINDOW = 128
BLOCK = 128
ROPE_THETA = 10000.0
D_FF = 5632
CONV_W = 3
EPS = 1e-6
NEG = -1e30

kernel_name = "hymba_s5_swa_convffn_step"


def rmsnorm(x, g):
    xf = x.astype(jnp.float32)
    y = xf * lax.rsqrt(jnp.mean(xf * xf, axis=-1, keepdims=True) + EPS) * g.astype(jnp.float32)
    return y.astype(x.dtype)


def rope(x, pos):
    half = HEAD_DIM // 2
    inv = ROPE_THETA ** (-jnp.arange(half, dtype=jnp.float32) * 2.0 / HEAD_DIM)
    ang = pos.astype(jnp.float32)[:, None] * inv[None, :]
    cos = jnp.cos(ang)[:, None, :]
    sin = jnp.sin(ang)[:, None, :]
    xf = x.astype(jnp.float32)
    x1, x2 = xf[..., :half], xf[..., half:]
    return jnp.concatenate([x1 * cos - x2 * sin, x1 * sin + x2 * cos], axis=-1).astype(x.dtype)


def project(h, w_in, q_norm_g, k_norm_g, pos):
    bn, t, _ = h.shape
    z = h @ w_in
    u = z[..., :D_SSM]
    q = z[..., D_SSM:D_SSM + D_ATTN].reshape(bn, t, N_HEADS, HEAD_DIM)
    k = z[..., D_SSM + D_ATTN:D_SSM + D_ATTN + D_KV].reshape(bn, t, N_KV_HEADS, HEAD_DIM)
    v = z[..., D_SSM + D_ATTN + D_KV:].reshape(bn, t, N_KV_HEADS, HEAD_DIM)
    q = rope(rmsnorm(q, q_norm_g), pos)
    k = rope(rmsnorm(k, k_norm_g), pos)
    return u, q, k, v


def _cscan_combine(e1, e2):
    a1r, a1i, b1r, b1i = e1
    a2r, a2i, b2r, b2i = e2
    return (a2r * a1r - a2i * a1i,
            a2r * a1i + a2i * a1r,
            a2r * b1r - a2i * b1i + b2r,
            a2r * b1i + a2i * b1r + b2i)


def s5_mixer(u, h0_re, h0_im, lam_re, lam_im, log_dt, b_re, b_im, c_re, c_im, d_skip, w_glu, b_glu):
    bn, t, _ = u.shape
    uf = u.astype(jnp.float32).reshape(bn, t, N_SSM_GROUPS, SSM_GROUP)
    dt = jnp.exp(log_dt.astype(jnp.float32))[:, None]
    lr = lam_re.astype(jnp.float32)
    li = lam_im.astype(jnp.float32)
    mag = jnp.exp(lr * dt)
    a_re = mag * jnp.cos(li * dt)
    a_im = mag * jnp.sin(li * dt)
    den = lr * lr + li * li
    am1 = a_re - 1.0
    coef_re = (am1 * lr + a_im * li) / den
    coef_im = (a_im * lr - am1 * li) / den
    bu_re = jnp.einsum('btgh,gph->btgp', uf, b_re.astype(jnp.float32))
    bu_im = jnp.einsum('btgh,gph->btgp', uf, b_im.astype(jnp.float32))
    x_re = coef_re * bu_re - coef_im * bu_im
    x_im = coef_re * bu_im + coef_im * bu_re
    h0r = h0_re.astype(jnp.float32)
    h0i = h0_im.astype(jnp.float32)
    x_re = x_re.at[:, 0].add(a_re * h0r - a_im * h0i)
    x_im = x_im.at[:, 0].add(a_re * h0i + a_im * h0r)
    a_re_b = jnp.broadcast_to(a_re, x_re.shape)
    a_im_b = jnp.broadcast_to(a_im, x_im.shape)
    _, _, s_re, s_im = lax.associative_scan(_cscan_combine, (a_re_b, a_im_b, x_re, x_im), axis=1)
    y = (jnp.einsum('btgp,ghp->btgh', s_re, c_re.astype(jnp.float32))
         - jnp.einsum('btgp,ghp->btgh', s_im, c_im.astype(jnp.float32)))
    y = y + d_skip.astype(jnp.float32) * uf
    y = jax.nn.gelu(y.reshape(bn, t, D_SSM)).astype(u.dtype)
    y = y * jax.nn.sigmoid(y @ w_glu + b_glu)
    return y, s_re[:, -1], s_im[:, -1]


def _sink_softmax(s, sinks_b):
    sink_col = jnp.broadcast_to(sinks_b, s.shape[:-1] + (1,))
    p = jax.nn.softmax(jnp.concatenate([s, sink_col], axis=-1), axis=-1)
    return p[..., :-1]


def swa_prompt(q, k, v, sinks):
    bn, t = q.shape[:2]
    pad = (-t) % BLOCK
    tp = t + pad
    nb = tp // BLOCK
    padw = ((0, 0), (pad, 0), (0, 0), (0, 0))
    qb = jnp.pad(q, padw).reshape(bn, nb, BLOCK, N_KV_HEADS, Q_PER_KV, HEAD_DIM)
    kb = jnp.pad(k, padw).reshape(bn, nb, BLOCK, N_KV_HEADS, HEAD_DIM)
    vb = jnp.pad(v, padw).reshape(bn, nb, BLOCK, N_KV_HEADS, HEAD_DIM)
    shift = ((0, 0), (1, 0), (0, 0), (0, 0), (0, 0))
    k2 = jnp.concatenate([jnp.pad(kb, shift)[:, :-1], kb], axis=2)
    v2 = jnp.concatenate([jnp.pad(vb, shift)[:, :-1], vb], axis=2)
    s = jnp.einsum('bnqkgd,bnskd->bnkgqs', qb, k2).astype(jnp.float32) * (HEAD_DIM ** -0.5)
    blk = jnp.arange(nb)[:, None] * BLOCK
    qi = blk + jnp.arange(BLOCK)[None, :]
    ki = blk - BLOCK + jnp.arange(2 * BLOCK)[None, :]
    diff = qi[:, :, None] - ki[:, None, :]
    mask = (diff >= 0) & (diff < WINDOW) & (ki[:, None, :] >= pad)
    s = jnp.where(mask[None, :, None, None], s, NEG)
    sk = sinks.astype(jnp.float32).reshape(N_KV_HEADS, Q_PER_KV)[None, None, :, :, None, None]
    p = _sink_softmax(s, sk)
    o = jnp.einsum('bnkgqs,bnskd->bnqkgd', p.astype(v.dtype), v2)
    return o.reshape(bn, tp, D_ATTN)[:, pad:]


def swa_sample(q, k, v, ck, cv, sinks):
    bn, s_len = q.shape[:2]
    k_all = jnp.concatenate([ck.astype(k.dtype), k], axis=1)
    v_all = jnp.concatenate([cv.astype(v.dtype), v], axis=1)
    qg = q.reshape(bn, s_len, N_KV_HEADS, Q_PER_KV, HEAD_DIM)
    s = jnp.einsum('bqkgd,bskd->bkgqs', qg, k_all).astype(jnp.float32) * (HEAD_DIM ** -0.5)
    qpos = PAST_LEN + jnp.arange(s_len)
    kpos = PAST_LEN - WINDOW + jnp.arange(WINDOW + s_len)
    diff = qpos[:, None] - kpos[None, :]
    mask = (diff >= 0) & (diff < WINDOW)
    s = jnp.where(mask, s, NEG)
    sk = sinks.astype(jnp.float32).reshape(N_KV_HEADS, Q_PER_KV)[None, :, :, None, None]
    p = _sink_softmax(s, sk)
    o = jnp.einsum('bkgqs,bskd->bqkgd', p.astype(v.dtype), v_all).reshape(bn, s_len, D_ATTN)
    return o, k_all[:, -WINDOW:], v_all[:, -WINDOW:]


def conv_ffn(h, buf, w_up, conv_w, conv_b, w_down):
    t = h.shape[1]
    up = h @ w_up
    xp = jnp.concatenate([buf.astype(up.dtype), up], axis=1)
    c = conv_b
    for j in range(CONV_W):
        c = c + conv_w[j] * xp[:, j:j + t]
    val, gate = c[..., :D_FF], c[..., D_FF:]
    out = (jax.nn.silu(gate) * val) @ w_down
    return out, xp[:, -(CONV_W - 1):]


def block(x, pos, attn_cache, h0_re, h0_im, conv_buf, norm_mix_g, w_in, q_norm_g, k_norm_g, attn_sinks,
          lam_re, lam_im, log_dt, ssm_b_re, ssm_b_im, ssm_c_re, ssm_c_im, ssm_d, w_glu, b_glu,
          ssm_out_g, attn_out_g, w_out, norm_ffn_g, w_up, conv_w, conv_b, w_down):
    h = rmsnorm(x, norm_mix_g)
    u, q, k, v = project(h, w_in, q_norm_g, k_norm_g, pos)
    y_ssm, s_re, s_im = s5_mixer(u, h0_re, h0_im, lam_re, lam_im, log_dt, ssm_b_re, ssm_b_im,
                                 ssm_c_re, ssm_c_im, ssm_d, w_glu, b_glu)
    if attn_cache is None:
        y_att = swa_prompt(q, k, v, attn_sinks)
        nk, nv = k[:, -WINDOW:], v[:, -WINDOW:]
    else:
        y_att, nk, nv = swa_sample(q, k, v, attn_cache[0], attn_cache[1], attn_sinks)
    mixed = jnp.concatenate([rmsnorm(y_ssm, ssm_out_g), rmsnorm(y_att.astype(x.dtype), attn_out_g)], axis=-1)
    x = x + mixed @ w_out
    f, new_buf = conv_ffn(rmsnorm(x, norm_ffn_g), conv_buf, w_up, conv_w, conv_b, w_down)
    x = x + f
    return x, nk, nv, s_re, s_im, new_buf


def setup_inputs(seed: int = 0) -> dict:
    key = jax.random.key(seed)
    ks = iter(jax.random.split(key, 40))
    f32 = jnp.float32

    def nrm(shape, scale):
        return jax.random.normal(next(ks), shape, f32) * scale

    G, P, H = N_SSM_GROUPS, SSM_STATE, SSM_GROUP
    d = {}
    d['x_prompt'] = nrm((BATCH, SEQ, D_MODEL), 1.0)
    d['x_sample'] = nrm((DEC_BATCH, DEC_SEQ, D_MODEL), 1.0)
    d['cache_k'] = nrm((DEPTH, DEC_BATCH, WINDOW, N_KV_HEADS, HEAD_DIM), 1.0)
    d['cache_v'] = nrm((DEPTH, DEC_BATCH, WINDOW, N_KV_HEADS, HEAD_DIM), 1.0)
    d['state_ssm_re'] = nrm((DEPTH, DEC_BATCH, G, P), 0.1)
    d['state_ssm_im'] = nrm((DEPTH, DEC_BATCH, G, P), 0.1)
    d['state_conv'] = nrm((DEPTH, DEC_BATCH, CONV_W - 1, 2 * D_FF), 1.0)
    d['meta_tokens'] = nrm((N_META, D_MODEL), 1.0)
    d['norm_mix_g'] = 1.0 + nrm((DEPTH, D_MODEL), 0.02)
    d['w_in'] = nrm((DEPTH, D_MODEL, D_IN), D_MODEL ** -0.5)
    d['q_norm_g'] = 1.0 + nrm((DEPTH, HEAD_DIM), 0.02)
    d['k_norm_g'] = 1.0 + nrm((DEPTH, HEAD_DIM), 0.02)
    d['attn_sinks'] = nrm((DEPTH, N_HEADS), 0.5)
    d['lam_re'] = -0.5 + nrm((DEPTH, G, P), 0.01)
    d['lam_im'] = math.pi * jnp.arange(P, dtype=f32)[None, None, :] + nrm((DEPTH, G, P), 0.01)
    d['log_dt'] = jax.random.uniform(next(ks), (DEPTH, G), f32, math.log(1e-3), math.log(1e-1))
    d['ssm_b_re'] = nrm((DEPTH, G, P, H), (2 * H) ** -0.5)
    d['ssm_b_im'] = nrm((DEPTH, G, P, H), (2 * H) ** -0.5)
    d['ssm_c_re'] = nrm((DEPTH, G, H, P), (2 * P) ** -0.5)
    d['ssm_c_im'] = nrm((DEPTH, G, H, P), (2 * P) ** -0.5)
    d['ssm_d'] = nrm((DEPTH, G, H), 1.0)
    d['w_glu'] = nrm((DEPTH, D_SSM, D_SSM), D_SSM ** -0.5)
    d['b_glu'] = nrm((DEPTH, D_SSM), 0.01)
    d['ssm_out_g'] = 1.0 + nrm((DEPTH, D_SSM), 0.02)
    d['attn_out_g'] = 1.0 + nrm((DEPTH, D_ATTN), 0.02)
    d['w_out'] = nrm((DEPTH, D_SSM + D_ATTN, D_MODEL), (D_SSM + D_ATTN) ** -0.5)
    d['norm_ffn_g'] = 1.0 + nrm((DEPTH, D_MODEL), 0.02)
    d['w_up'] = nrm((DEPTH, D_MODEL, 2 * D_FF), D_MODEL ** -0.5)
    d['conv_w'] = nrm((DEPTH, CONV_W, 2 * D_FF), CONV_W ** -0.5)
    d['conv_b'] = nrm((DEPTH, 2 * D_FF), 0.01)
    d['w_down'] = nrm((DEPTH, D_FF, D_MODEL), D_FF ** -0.5)
    return d


def reference(x_prompt, x_sample, cache_k, cache_v, state_ssm_re, state_ssm_im, state_conv,
              meta_tokens, norm_mix_g, w_in, q_norm_g, k_norm_g, attn_sinks, lam_re, lam_im, log_dt,
              ssm_b_re, ssm_b_im, ssm_c_re, ssm_c_im, ssm_d, w_glu, b_glu, ssm_out_g, attn_out_g,
              w_out, norm_ffn_g, w_up, conv_w, conv_b, w_down):
    xp = jnp.concatenate([jnp.broadcast_to(meta_tokens.astype(x_prompt.dtype)[None], (BATCH, N_META, D_MODEL)),
                          x_prompt], axis=1)
    pos_p = jnp.arange(N_META + SEQ)
    xs = x_sample
    pos_s = PAST_LEN + jnp.arange(DEC_SEQ)
    zero_re = jnp.zeros((BATCH, N_SSM_GROUPS, SSM_STATE), jnp.float32)
    zero_buf = jnp.zeros((BATCH, CONV_W - 1, 2 * D_FF), x_prompt.dtype)

    pk, pv, pre, pim, pconv = [], [], [], [], []
    sk, sv, sre, sim, sconv = [], [], [], [], []
    for l in range(DEPTH):
        w_l = (norm_mix_g[l], w_in[l], q_norm_g[l], k_norm_g[l], attn_sinks[l], lam_re[l], lam_im[l], log_dt[l],
               ssm_b_re[l], ssm_b_im[l], ssm_c_re[l], ssm_c_im[l], ssm_d[l], w_glu[l], b_glu[l],
               ssm_out_g[l], attn_out_g[l], w_out[l], norm_ffn_g[l], w_up[l], conv_w[l], conv_b[l], w_down[l])
        xp, k1, v1, r1, i1, c1 = block(xp, pos_p, None, zero_re, zero_re, zero_buf, *w_l)
        pk.append(k1); pv.append(v1); pre.append(r1); pim.append(i1); pconv.append(c1)
        xs, k2, v2, r2, i2, c2 = block(xs, pos_s, (cache_k[l], cache_v[l]), state_ssm_re[l], state_ssm_im[l],
                                       state_conv[l], *w_l)
        sk.append(k2); sv.append(v2); sre.append(r2); sim.append(i2); sconv.append(c2)

    y_prompt = xp[:, N_META:]
    y_sample = xs
    return (y_prompt, y_sample,
            jnp.stack(pk), jnp.stack(pv), jnp.stack(pre), jnp.stack(pim), jnp.stack(pconv),
            jnp.stack(sk), jnp.stack(sv), jnp.stack(sre), jnp.stack(sim), jnp.stack(sconv))
```

```python
import contextlib
import math
import numpy as np
import ml_dtypes
import concourse.bass as bass
import concourse.mybir as mybir
from concourse.bass_utils import run_bass_kernel_spmd

F32 = mybir.dt.float32
BF16 = mybir.dt.bfloat16
I32 = mybir.dt.int32
ALU = mybir.AluOpType
AF = mybir.ActivationFunctionType

ENGS = ['pe', 'act', 'dve', 'pool', 'sp']
NDMASEM = 8
EPS = 1e-6
TWO_PI = 2.0 * math.pi
MAGIC = 12582912.0


class Res:
    __slots__ = ('name', 'w', 'rs', 'kids')

    def __init__(self, name='', kids=None):
        self.name = name
        self.w = None
        self.rs = []
        self.kids = kids


def _flat(lst):
    out = []
    for x in lst:
        if x.kids:
            out.extend(x.kids)
        else:
            out.append(x)
    return out


class Op:
    __slots__ = ('eng', 'fn', 'deps', 'signal', 'sem', 'val', 'ndma', 'idx', 'calls')


class Rec:
    def __init__(self):
        self.calls = []

    def __getattr__(self, name):
        def f(*a, **k):
            self.calls.append((name, a, k))
            return None
        return f


class Prog:
    def __init__(self, nc, es):
        self.nc = nc
        self.es = es
        self.ops = {e: [] for e in ENGS}
        self.n = 0
        self.csem = {e: es.enter_context(nc.semaphore('c_' + e)) for e in ENGS}
        self.dsem = {e: [es.enter_context(nc.semaphore('d_%s%d' % (e, i))) for i in range(NDMASEM)]
                     for e in ('sp', 'pool', 'act')}
        self.dtot = {e: [0] * NDMASEM for e in ('sp', 'pool', 'act')}
        self.dlast = {e: [None] * NDMASEM for e in ('sp', 'pool', 'act')}
        self.drr = {e: 0 for e in ('sp', 'pool', 'act')}

    def sb(self, name, shape, dt):
        return self.es.enter_context(self.nc.sbuf_tensor(name, list(shape), dt))

    def ps(self, name, shape, dt=F32):
        return self.es.enter_context(self.nc.psum_tensor(name, list(shape), dt))

    def op(self, eng, fn, r=(), w=(), ndma=0):
        o = Op()
        o.eng = eng
        o.fn = fn
        rec = Rec()
        fn(rec)
        o.calls = rec.calls
        o.ndma = ndma
        o.signal = False
        o.sem = None
        o.val = 0
        o.idx = self.n
        self.n += 1
        deps = {}
        r = _flat(r)
        w = _flat(w)
        w = list(w) + [x for x in r if x.name.startswith('pp') and x not in w]
        r = [x for x in r if not x.name.startswith('pp')]
        for x in r:
            if x.w is not None:
                deps[x.w.idx] = x.w
        for x in w:
            if x.w is not None:
                deps[x.w.idx] = x.w
            for q in x.rs:
                deps[q.idx] = q
        if ndma:
            k = self.drr[eng]
            self.drr[eng] = (k + 1) % NDMASEM
            prev = self.dlast[eng][k]
            if prev is not None:
                deps[prev.idx] = prev
            self.dtot[eng][k] += 16 * ndma
            o.sem = self.dsem[eng][k]
            o.val = self.dtot[eng][k]
            o.signal = True
            self.dlast[eng][k] = o
        dl = []
        for d in deps.values():
            if d.eng == 'pe' and eng == 'pe' and not d.ndma and not ndma:
                continue
            dl.append(d)
        o.deps = dl
        for x in r:
            x.rs.append(o)
        for x in w:
            x.w = o
            x.rs = []
        self.ops[eng].append(o)
        return o

    def dma(self, eng, out, in_, r=(), w=(), **kw):
        return self.op(eng, lambda e: [e.dma_start(out=out, in_=in_, **kw)], r=r, w=w, ndma=1)

    def emit(self):
        nc = self.nc
        for e in ENGS:
            for o in self.ops[e]:
                for d in o.deps:
                    d.signal = True
        finals = []
        for e in ENGS:
            c = 0
            for o in self.ops[e]:
                if o.ndma:
                    continue
                if o.signal:
                    c += 1
                    o.val = c
                    o.sem = self.csem[e]
            if c and e != 'sp':
                finals.append((self.csem[e], c))
        for e in ('sp', 'pool', 'act'):
            for k in range(NDMASEM):
                if self.dtot[e][k]:
                    finals.append((self.dsem[e][k], self.dtot[e][k]))

        def mk(e):
            def body(engobj):
                known = {}
                for o in self.ops[e]:
                    for d in o.deps:
                        key = id(d.sem)
                        if known.get(key, 0) >= d.val:
                            continue
                        engobj.wait_ge(d.sem, d.val)
                        known[key] = d.val
                    ins = [getattr(engobj, nm)(*a, **k) for nm, a, k in o.calls]
                    if o.ndma:
                        assert len(ins) == o.ndma
                        for i in ins:
                            i.then_inc(o.sem, 16)
                    elif o.signal:
                        ins[-1].then_inc(o.sem, 1)
                if e == 'sp':
                    for s, v in finals:
                        engobj.wait_ge(s, v)
            return body

        with nc.Block() as block:
            block.tensor(mk('pe'))
            block.scalar(mk('act'))
            block.vector(mk('dve'))
            block.gpsimd(mk('pool'))
            block.sync(mk('sp'))


D = 2048
DFF = 5632
NFC = 44
NCOL = [560, 512, 512, 512]
QHEAD_ORDER = [0, 4, 1, 5, 2, 6, 3, 7, 8, 12, 9, 13, 10, 14, 11, 15]

IN_SPECS = [
    ("xp", [2048, 2048]), ("xs", [16, 2048]), ("meta", [16, 2048]),
    ("ck", [16, 128, 256]), ("cv", [16, 128, 256]), ("sre", [16, 64, 64]), ("sim", [16, 64, 64]),
    ("sconv", [16, 2, 11264]),
    ("g_mix", [128, 16]), ("g_ffn", [128, 16]), ("w_in", [2048, 2560]), ("qg", [128, 1]), ("kg", [128, 1]),
    ("sinks", [1, 16]), ("lamre2", [128, 64]), ("lamim2", [128, 64]), ("logdt2", [128, 64]),
    ("bre_t", [64, 64, 16]), ("bim_t", [64, 64, 16]), ("cre_t", [64, 64, 16]), ("cim_t", [64, 64, 16]),
    ("dsk", [128, 8]), ("w_glu", [1024, 1024]), ("b_glu", [128, 8]), ("g_s", [128, 8]), ("g_a", [128, 8]),
    ("w_out", [2048, 2048]), ("w_up", [2048, 11264]), ("cw", [128, 3, 88]), ("cb", [128, 88]),
    ("w_down", [5632, 2048]),
    ("ident", [128, 128]), ("rotm", [128, 128]), ("blk64", [128, 128]), ("ones", [128, 128]),
    ("maskc", [128, 128]), ("maskp", [128, 128]), ("m48c", [128, 128]), ("m48p", [128, 128]),
    ("posc", [4, 560]), ("ropeinv", [128, 1]),
]
OUT_SPECS = [
    ("y_p", [2048, 2048]), ("y_s", [16, 2048]), ("pck", [128, 256]), ("pcv", [128, 256]),
    ("pre", [64, 64]), ("pim", [64, 64]), ("pconv", [2, 11264]),
    ("sck", [16, 128, 256]), ("scv", [16, 128, 256]), ("ssre", [16, 64, 64]), ("ssim", [16, 64, 64]),
    ("ssconv", [16, 2, 11264]),
]


def st_tiles(s):
    if s == 0:
        return [(0, 0, 48)] + [(i + 1, 48 + 128 * i, 128) for i in range(4)]
    return [(i, 128 * i, 128) for i in range(4)]


def st_chunks(s):
    return [(0, 48), (48, 512)] if s == 0 else [(0, 512)]


CFG = dict(nst=4, stage=99, taps=())
TAPS = {}


def build_nc():
    nc = bass.Bass("TRN2", target_bir_lowering=False)
    TAPS.clear()
    I = {n: nc.dram_tensor(n, sh, F32, kind="ExternalInput").ap() for n, sh in IN_SPECS}
    O = {n: nc.dram_tensor(n, sh, F32, kind="ExternalOutput").ap() for n, sh in OUT_SPECS}
    with contextlib.ExitStack() as es:
        P = Prog(nc, es)
        build_body(nc, P, I, O)
        P.emit()
    return nc


def build_body(nc, P, I, O):
    op = P.op
    R = {}

    def tap(name, ap, rr):
        if name.rstrip("0123456789") not in CFG["taps"]:
            return
        shape = list(ap.shape)
        d = nc.dram_tensor("tap_" + name, shape, F32, kind="ExternalOutput").ap()
        TAPS[name] = shape
        idx = tuple(slice(None) for _ in shape)
        P.dma('pool', d[idx], ap, r=rr)

    def res(n):
        if n not in R:
            R[n] = Res(n)
        return R[n]

    xres = P.sb("xres", [128, 5, 2048], F32)
    r_x = [res("x%d" % i) for i in range(5)]
    hT = P.sb("hT", [128, 16, 560], BF16)
    r_hT = res("hT")
    NW = 5
    wb = [P.sb("wb%d" % i, [128, 4096], BF16) for i in range(NW)]
    r_wb = [res("wb%d" % i) for i in range(NW)]
    wrr = [0]
    uT = P.sb("uT", [128, 8, 560], BF16)
    r_u = [res("u%d" % i) for i in range(8)]
    QT = P.sb("QT", [128, 8, 560], BF16)
    r_Q = res("QT")
    KT = P.sb("KT", [128, 2, 688], BF16)
    r_K = res("KT")
    V1 = P.sb("V1", [128, 6, 4, 65], BF16)
    r_V = [res("V%d" % i) for i in range(6)]
    NS = 8
    scr = [P.sb("scr%d" % i, [128, 562], F32) for i in range(NS)]
    r_s = [res("scr%d" % i) for i in range(NS)]
    W1b = P.sb("W1b", [128, 560], BF16)
    W2b = P.sb("W2b", [128, 560], BF16)
    r_W1, r_W2 = res("W1b"), res("W2b")
    W1x = [W1b, P.sb("W1b2", [128, 560], BF16)]
    W2x = [W2b, P.sb("W2b2", [128, 560], BF16)]
    r_W1x = [r_W1, res("W1b2")]
    r_W2x = [r_W2, res("W2b2")]
    sqb = P.sb("sqb", [128, 2048], BF16)
    r_sqb = res("sqb")
    actTx = [P.sb("actT%d" % i, [128, 2, 560], BF16) for i in range(2)]
    r_actTx = [res("actT%d" % i) for i in range(2)]
    rcos_t = P.sb("rcos", [128, 562], F32)
    rsin_t = P.sb("rsin", [128, 562], F32)
    rcos, rsin = rcos_t, rsin_t
    scr += [rcos_t, rsin_t]
    r_s += [res("rcos"), res("rsin")]
    TI = P.sb("TI", [128, 560], F32)
    rstdb = P.sb("rstdb", [128, 560], F32)
    r_rope = Res("rope", kids=[r_s[8], r_s[9]])
    r_TI, r_rstdb = res("TI"), res("rstdb")
    Bst = [P.sb("Bst%d" % i, [128, 16, 128], BF16) for i in range(2)]
    Cst = [P.sb("Cst%d" % i, [128, 64, 32], BF16) for i in range(2)]
    Dd = P.sb("Dd", [128, 8, 128], BF16)
    r_ssmc = res("ssmc")
    r_zl = res("zl")
    r_sck, r_scv = res("sck"), res("scv")
    tab = {k: P.sb("tab_" + k, [128, 64], F32) for k in
           ["lr", "li", "dt", "th", "mag", "ar", "ai", "cre", "cim", "cres", "cims", "t1", "t2", "t3", "zl"]}
    att = P.sb("att", [128, 16, 64], F32)
    attn = P.sb("attn", [128, 1024], BF16)
    r_att, r_attn = res("att"), res("attn")
    PTc = P.sb("PTc", [128, 4, 128], BF16)
    PTp = P.sb("PTp", [128, 4, 128], BF16)
    r_PTc, r_PTp = res("PTc"), res("PTp")
    fin = P.sb("fin", [128, 128], F32)
    r_fin = res("fin")
    small = P.sb("small", [128, 64], F32)
    r_small = res("small")
    esink = P.sb("esink", [128, 16], F32)
    cst = {k: P.sb("c_" + k, [128, 128], F32) for k in ["ident"]}
    cbf = {k: P.sb("b_" + k, [128, 128], BF16) for k in
           ["ident", "rotm", "blk64", "ones", "maskc", "maskp", "m48c", "m48p"]}
    vec = {k: P.sb("v_" + k, sh, F32) for k, sh in
           [("g_mix", [128, 16]), ("g_ffn", [128, 16]), ("qg", [128, 1]), ("kg", [128, 1]), ("dsk", [128, 8]),
            ("b_glu", [128, 8]), ("g_s", [128, 8]), ("g_a", [128, 8]), ("cw", [128, 3, 88]), ("cb", [128, 88]),
            ("ropeinv", [128, 1])]}
    r_c = res("consts")
    convst = P.sb("convst", [128, 88, 2], F32)
    r_convst = res("convst")
    stT = P.sb("stT", [128, 4, 32], F32)
    r_stT = res("stT")
    h0tok = att[0:16].rearrange("p a b -> p (a b)").rearrange("p (a b) -> p a b", a=8)
    h0tok2 = sqb[0:16, :].bitcast(F32).rearrange("p (a b) -> p a b", a=8)
    h0T = P.sb("h0T", [128, 8, 16], F32)
    h0T2 = P.sb("h0T2", [128, 8, 16], F32)
    sT = P.sb("sT", [128, 8, 16], F32)
    r_h0 = r_att
    R["sqb"] = r_att
    r_sqb = r_att
    KVb = P.sb("KVb", [128, 4160], BF16)
    Kb = KVb[:, 0:2048].rearrange("p (a b) -> p a b", a=8)
    Vb = KVb[:, 2048:4128].rearrange("p (a b c) -> p a b c", a=8, b=4)
    KbT = W2b[:, 0:256].rearrange("p (a b) -> p a b", a=2)
    r_Kb = res("Kb")
    r_Vb, r_KbT = r_Kb, r_W2
    tokst = att[0:32].rearrange("p a b -> p (a b)")[:, 0:512]
    r_tokst = r_att
    wb.append(KVb[:, 0:4096])
    r_wb.append(r_Kb)
    PTs = W1b[:, 0:256]
    r_PTs = r_W1
    OTs = scr[7][0:65, 0:256]
    r_OTs = r_s[7]
    outst = P.sb("outst", [128, 512], F32)
    r_outst = res("outst")
    pp = [P.ps("pp%d" % i, [128, 1024], F32) for i in range(4)]
    r_ppb = [[res("pp%d_%d" % (i, b)) for b in range(2)] for i in range(4)]
    r_pp = [Res("ppc%d" % i, kids=r_ppb[i]) for i in range(4)]

    def pbf(i):
        return pp[i][:].bitcast(BF16)

    for k in cst:
        P.dma('sp', cst[k][:], I[k][:, :], w=[r_c])
    for k in cbf:
        P.dma('pool', cbf[k][:], I[k][:, :], w=[r_c])
    for k in vec:
        src = I[k]
        P.dma('sp', vec[k][:], src[:] if len(src.shape) == 2 else src[:, :, :], w=[r_c])
    P.dma('sp', esink[:], I["sinks"].partition_broadcast(128), w=[r_c])
    op('act', lambda e: e.activation(out=esink[:], in_=esink[:], func=AF.Exp), r=[r_c], w=[r_c])
    op('dve', lambda e: e.memset(small[:], 0.0), w=[r_small])
    op('dve', lambda e: e.memset(small[:, 60:61], EPS), r=[r_small], w=[r_small])
    op('dve', lambda e: e.memset(small[:, 61:62], math.pi / 2), r=[r_small], w=[r_small])
    op('dve', lambda e: e.memset(small[:, 58:59], MAGIC), r=[r_small], w=[r_small])
    op('dve', lambda e: e.memset(small[:, 59:60], -MAGIC), r=[r_small], w=[r_small])
    op('dve', lambda e: e.tensor_scalar(out=small[:, 62:63], in0=vec["ropeinv"][:], scalar1=1.0 / TWO_PI, scalar2=None,
                                          op0=ALU.mult), r=[r_small, r_c], w=[r_small])
    EPS_AP = small[:, 60:61]
    HPI_AP = small[:, 61:62]
    RINV_AP = small[:, 62:63]
    M_AP = small[:, 58:59]
    NM_AP = small[:, 59:60]
    op('dve', lambda e: e.memset(convst[:], 0.0), w=[r_convst])
    op('pool', lambda e: e.memset(V1[:], 1.0), w=r_V)
    op('pool', lambda e: e.memset(Vb, 1.0), w=[r_Vb])
    op('pool', lambda e: e.memset(KT[:], 0.0), w=[r_K])

    def sincos(src_ap, scale_ap, n, cos_out, sin_out, rr, ww, tmp):
        a, b = tmp
        op('act', lambda e: e.activation(out=scr[a][:, :n], in_=src_ap, func=AF.Identity, scale=scale_ap, bias=M_AP),
           r=rr + [r_small, r_ssmc], w=[r_s[a]])
        op('act', lambda e: e.activation(out=scr[a][:, :n], in_=scr[a][:, :n], func=AF.Identity, bias=NM_AP),
           r=[r_s[a], r_small], w=[r_s[a]])
        op('dve', lambda e: e.scalar_tensor_tensor(out=scr[b][:, :n], in0=src_ap, scalar=scale_ap, in1=scr[a][:, :n],
                                                    op0=ALU.mult, op1=ALU.subtract), r=rr + [r_s[a], r_small, r_ssmc], w=[r_s[b]])
        op('act', lambda e: e.activation(out=sin_out, in_=scr[b][:, :n], func=AF.Sin, scale=TWO_PI),
           r=[r_s[b]], w=ww)
        op('act', lambda e: e.activation(out=scr[a][:, :n], in_=scr[b][:, :n], func=AF.Abs), r=[r_s[b]], w=[r_s[a]])
        op('act', lambda e: e.activation(out=cos_out, in_=scr[a][:, :n], func=AF.Sin, scale=-TWO_PI, bias=HPI_AP),
           r=[r_s[a], r_small], w=ww)

    def wload(srcs):
        k = wrr[0]
        wrr[0] = (k + 1) % NW
        buf = wb[k]
        for f, s in srcs:
            P.dma('pool', f(buf), s, w=[r_wb[k]])
        return buf, r_wb[k]

    wv_ = I["w_in"].rearrange("(k p) n -> p k n", p=128)
    wg_ = I["w_glu"].rearrange("(k p) n -> p k n", p=128)
    wo_ = I["w_out"].rearrange("(k p) n -> p k n", p=128)
    wu_ = I["w_up"].rearrange("(k p) n -> p k n", p=128)
    wd_ = I["w_down"].rearrange("(k p) n -> p k n", p=128)
    sc_in = nc.dram_tensor("sc_in", [10, 128, 4096], BF16).ap()
    sc_glu = nc.dram_tensor("sc_glu", [4, 128, 2048], BF16).ap()
    sc_out = nc.dram_tensor("sc_out", [8, 128, 4096], BF16).ap()
    sc_ffn = nc.dram_tensor("sc_ffn", [66, 128, 4096], BF16).ap()
    r_sc = {}
    pc_list = []
    for pi in range(10):
        pc_list.append((sc_in[pi].rearrange("p (k n) -> p k n", k=16), wv_[:, :, 256 * pi:256 * (pi + 1)], ('in', pi)))
    for i4 in range(4):
        pc_list.append((sc_glu[i4].rearrange("p (k n) -> p k n", k=8), wg_[:, :, 256 * i4:256 * (i4 + 1)], ('glu', i4)))
    for pi in range(8):
        pc_list.append((sc_out[pi].rearrange("p (k n) -> p k n", k=16), wo_[:, :, 256 * pi:256 * (pi + 1)], ('out', pi)))
    for g in range(22):
        pc_list.append((sc_ffn[3 * g].rearrange("p (k n) -> p k n", k=16), wu_[:, :, 256 * g:256 * (g + 1)], ('ffn', 3 * g)))
        pc_list.append((sc_ffn[3 * g + 1].rearrange("p (k n) -> p k n", k=16), wu_[:, :, DFF + 256 * g:DFF + 256 * (g + 1)], ('ffn', 3 * g + 1)))
        pc_list.append((sc_ffn[3 * g + 2].rearrange("p (k n) -> p k n", k=2), wd_[:, 2 * g:2 * g + 2, :], ('ffn', 3 * g + 2)))
    for (_, _, key) in pc_list:
        r_sc[key] = res("sc_%s%d" % key)

    def emit_precast(n):
        for _ in range(n):
            if not pc_list:
                return
            dst, src, key = pc_list.pop(0)
            P.dma('pool', dst, src, w=[r_sc[key]])

    qtog = [0]

    def wfill(s, buf, rbuf, sc2d, key, width, src3d, kk):
        if s == 0:
            P.dma('pool', buf[:, 0:width].rearrange("p (k n) -> p k n", k=kk), src3d, w=[rbuf])
            P.dma('sp', sc2d, buf[:, 0:width], r=[rbuf], w=[r_sc[key]])
        else:
            q = ('sp', 'pool')[qtog[0]]
            qtog[0] = 1 - qtog[0]
            P.dma(q, buf[:, 0:width], sc2d, r=[r_sc[key]], w=[rbuf])

    bg_list = []
    for g in range(22):
        bg_list.append((sc_ffn[3 * g], ('ffn', 3 * g), wu_[:, :, 256 * g:256 * (g + 1)], 16))
        bg_list.append((sc_ffn[3 * g + 1], ('ffn', 3 * g + 1), wu_[:, :, DFF + 256 * g:DFF + 256 * (g + 1)], 16))
        bg_list.append((sc_ffn[3 * g + 2], ('ffn', 3 * g + 2), wd_[:, 2 * g:2 * g + 2, :], 2))
    bg_rr = [0]

    def bg_precast(n, flush=False):
        for _ in range(n):
            if not bg_list:
                return
            sc2d, key, src3d, kk = bg_list.pop(0)
            b = 3 + bg_rr[0]
            bg_rr[0] = 1 - bg_rr[0]
            P.dma('pool', wb[b][:].rearrange("p (k n) -> p k n", k=kk), src3d, w=[r_wb[b]])
            P.dma('sp', sc2d, wb[b][:], r=[r_wb[b]], w=[r_sc[key]])

    def wload_sc(s, sc2d, key, width, src3d, kk):
        k = wrr[0]
        wrr[0] = (k + 1) % 3
        wfill(s, wb[k], r_wb[k], sc2d, key, width, src3d, kk)
        return wb[k], r_wb[k]

    mix_specs = []
    for pi in range(10):
        mix_specs.append((sc_in[pi], ('in', pi), 4096, wv_[:, :, 256 * pi:256 * (pi + 1)], 16))
    for i4 in range(4):
        mix_specs.append((sc_glu[i4], ('glu', i4), 2048, wg_[:, :, 256 * i4:256 * (i4 + 1)], 8))
    for pi in range(8):
        mix_specs.append((sc_out[pi], ('out', pi), 4096, wo_[:, :, 256 * pi:256 * (pi + 1)], 16))
    mix_state = {}

    def mget(s, i):
        st = mix_state.setdefault(s, {'next': 0, 'got': {}})
        while st['next'] < min(len(mix_specs), i + 3):
            st['got'][st['next']] = wload_sc(s, *mix_specs[st['next']])
            st['next'] += 1
        return st['got'][i]

    T = tab
    P.dma('sp', T["lr"][:], I["lamre2"][:, :], w=[r_ssmc])
    P.dma('sp', T["li"][:], I["lamim2"][:, :], w=[r_ssmc])
    P.dma('sp', T["dt"][:], I["logdt2"][:, :], w=[r_ssmc])
    S1 = [r_ssmc]

    def tt(o, a, b, f, eng='dve'):
        op(eng, lambda e: e.tensor_tensor(out=o, in0=a, in1=b, op=f), r=S1, w=S1)

    op('act', lambda e: e.activation(out=T["dt"][:], in_=T["dt"][:], func=AF.Exp), r=S1, w=S1)
    tt(T["t1"][:], T["lr"][:], T["dt"][:], ALU.mult)
    op('act', lambda e: e.activation(out=T["mag"][:], in_=T["t1"][:], func=AF.Exp), r=S1, w=S1)
    tt(T["th"][:], T["li"][:], T["dt"][:], ALU.mult)
    op('dve', lambda e: e.tensor_scalar(out=T["th"][:], in0=T["th"][:], scalar1=1.0 / TWO_PI, scalar2=None, op0=ALU.mult),
       r=S1, w=S1)
    op('dve', lambda e: e.memset(scr[2][:, 0:1], 1.0), w=[r_s[2]])
    sincos(T["th"][:], scr[2][:, 0:1], 64, T["ar"][:], T["ai"][:], S1 + [r_s[2]], S1, (0, 1))
    tt(T["ar"][:], T["ar"][:], T["mag"][:], ALU.mult)
    tt(T["ai"][:], T["ai"][:], T["mag"][:], ALU.mult)
    tt(T["t1"][:], T["lr"][:], T["lr"][:], ALU.mult)
    tt(T["t2"][:], T["li"][:], T["li"][:], ALU.mult)
    tt(T["t1"][:], T["t1"][:], T["t2"][:], ALU.add)
    op('dve', lambda e: e.reciprocal(out=T["t1"][:], in_=T["t1"][:]), r=S1, w=S1)
    op('dve', lambda e: e.tensor_scalar(out=T["t2"][:], in0=T["ar"][:], scalar1=-1.0, scalar2=None, op0=ALU.add),
       r=S1, w=S1)
    tt(T["cre"][:], T["t2"][:], T["lr"][:], ALU.mult)
    tt(T["t3"][:], T["ai"][:], T["li"][:], ALU.mult)
    tt(T["cre"][:], T["cre"][:], T["t3"][:], ALU.add)
    tt(T["cre"][:], T["cre"][:], T["t1"][:], ALU.mult)
    tt(T["cim"][:], T["ai"][:], T["lr"][:], ALU.mult)
    tt(T["t3"][:], T["t2"][:], T["li"][:], ALU.mult)
    tt(T["cim"][:], T["cim"][:], T["t3"][:], ALU.subtract)
    tt(T["cim"][:], T["cim"][:], T["t1"][:], ALU.mult)
    op('dve', lambda e: e.tensor_copy(out=T["cres"][0:64, :], in_=T["cre"][0:64, :]), r=S1, w=S1)
    op('dve', lambda e: e.tensor_scalar(out=T["cres"][64:128, :], in0=T["cre"][64:128, :], scalar1=-1.0, scalar2=None,
                                          op0=ALU.mult), r=S1, w=S1)
    op('dve', lambda e: e.tensor_copy(out=T["cims"][64:128, :], in_=T["cim"][64:128, :]), r=S1, w=S1)
    op('dve', lambda e: e.tensor_scalar(out=T["cims"][0:64, :], in0=T["cim"][0:64, :], scalar1=-1.0, scalar2=None,
                                          op0=ALU.mult), r=S1, w=S1)
    op('dve', lambda e: e.memset(T["zl"][:], 0.0), r=S1, w=S1)
    S1 = [r_ssmc, r_wb[0], r_wb[1], r_wb[2]]
    TB1 = wb[0][:, 0:2048].bitcast(F32).rearrange("p (a b) -> p a b", b=16)
    TB2 = wb[0][:, 2048:4096].bitcast(F32).rearrange("p (a b) -> p a b", b=16)
    BS = wb[1][:, 0:2048].bitcast(F32).rearrange("p (a b) -> p a b", b=16)
    Bin = wb[2][:, 0:4096].bitcast(F32).rearrange("p (a b) -> p a b", b=128)
    P.dma('sp', TB1[0:64], I["bre_t"][:, :, :], w=S1)
    P.dma('sp', TB1[64:128], I["bim_t"][:, :, :], w=S1)
    P.dma('sp', TB2[0:64], I["bim_t"][:, :, :], w=S1)
    P.dma('sp', TB2[64:128], I["bre_t"][:, :, :], w=S1)

    def bc(t):
        return t[:].unsqueeze(2).to_broadcast([128, 64, 16])

    for v in range(2):
        ca, cb_ = (T["cre"], T["cims"]) if v == 0 else (T["cim"], T["cres"])
        tt(BS, TB1, bc(ca), ALU.mult)
        tt(TB1 if False else Bin.rearrange("p a (b c) -> p (a b) c", c=16)[:, 0:64, :], TB2, bc(cb_), ALU.mult)
        tt(BS, BS, Bin.rearrange("p a (b c) -> p (a b) c", c=16)[:, 0:64, :], ALU.add)
        op('dve', lambda e: e.memset(Bin, 0.0), r=S1, w=S1)
        for s_ in range(2):
            src = BS.rearrange("p (c j s) h -> p c j s h", c=8, j=4, s=2)[:, :, :, s_, :]
            dst = Bin.rearrange("p (c s) (j t h) -> p c s j t h", s=2, j=4, t=2)[:, :, s_, :, s_, :]
            op('dve', lambda e, src=src, dst=dst: e.tensor_copy(out=dst, in_=src), r=S1, w=S1)
        for q4 in range(4):
            for j in range(4):
                cs = q4 * 4 + j
                op('pe', lambda e, cs=cs, j=j: e.transpose(pp[0][:, 128 * j:128 * (j + 1)], Bin[:, cs, :], cst["ident"][:]),
                   r=S1 + [r_c], w=[r_pp[0]])
            op('act', lambda e, q4=q4, v=v: e.activation(out=Bst[v][:, 4 * q4:4 * q4 + 4, :],
                                                           in_=pp[0][:, 0:512].rearrange("p (a b) -> p a b", a=4), func=AF.Copy),
               r=[r_pp[0]], w=S1)
    P.dma('sp', TB1[0:64], I["cre_t"][:, :, :], r=S1, w=S1)
    P.dma('sp', TB1[64:128], I["cim_t"][:, :, :], r=S1, w=S1)
    P.dma('sp', TB2[0:64], I["cim_t"][:, :, :], r=S1, w=S1)
    P.dma('sp', TB2[64:128], I["cre_t"][:, :, :], r=S1, w=S1)
    op('dve', lambda e: e.tensor_scalar(out=TB1[64:128], in0=TB1[64:128], scalar1=-1.0, scalar2=None, op0=ALU.mult), r=S1, w=S1)
    op('dve', lambda e: e.tensor_scalar(out=TB2, in0=TB2, scalar1=-1.0, scalar2=None, op0=ALU.mult), r=S1, w=S1)
    for v, TBx in enumerate((TB1, TB2)):
        op('pool', lambda e, v=v: e.memset(Cst[v][:], 0.0), r=S1, w=S1)
        for s_ in range(2):
            src = TBx.rearrange("p (a s) h -> p a s h", s=2)[:, :, s_, :]
            dst = Cst[v][:].rearrange("p (a s) (t h) -> p a s t h", s=2, t=2)[:, :, s_, s_, :]
            op('dve', lambda e, src=src, dst=dst: e.tensor_copy(out=dst, in_=src), r=S1, w=S1)
    for c in range(8):
        op('dve', lambda e, c=c: e.tensor_scalar(out=Dd[:, c, :], in0=cbf["ident"][:], scalar1=vec["dsk"][:, c:c + 1],
                                                   scalar2=None, op0=ALU.mult), r=S1 + [r_c], w=S1)

    def norm_to_hT(s, gname):
        for (slot, c0, n) in st_tiles(s):
            op('dve', lambda e, n=n: e.memset(small[:n, 4:6], 0.0), r=[r_small], w=[r_small])
            for hh in range(2):
                op('act', lambda e, slot=slot, n=n, hh=hh: e.activation(out=pp[3][:n, 0:1024], in_=xres[:n, slot, 1024 * hh:1024 * (hh + 1)],
                                                                     func=AF.Square, accum_out=small[:n, 4 + hh:5 + hh]),
                   r=[r_x[slot], r_small], w=[r_pp[3], r_small])
            op('dve', lambda e, n=n: e.tensor_tensor(out=small[:n, 0:1], in0=small[:n, 4:5], in1=small[:n, 5:6], op=ALU.add),
               r=[r_small], w=[r_small])
            op('act', lambda e, n=n: e.activation(out=small[:n, 1:2], in_=small[:n, 0:1], func=AF.Sqrt, scale=1.0 / D,
                                                   bias=EPS_AP[:n]), r=[r_small], w=[r_small])
            op('dve', lambda e, n=n: e.reciprocal(out=small[:n, 1:2], in_=small[:n, 1:2]), r=[r_small], w=[r_small])
            op('act', lambda e, slot=slot, n=n: e.activation(out=sqb[:n, :], in_=xres[:n, slot, :], func=AF.Copy, scale=small[:n, 1:2]),
               r=[r_x[slot], r_small], w=[r_sqb])
            for q4 in range(4):
                pv = pbf(q4 % 2)
                for j in range(4):
                    k = q4 * 4 + j
                    op('pe', lambda e, k=k, j=j, pv=pv, n=n: e.transpose(
                        pv[:, 128 * j:128 * j + n], sqb[:n, 128 * k:128 * (k + 1)], cbf["ident"][:n, :n]),
                       r=[r_sqb, r_c], w=[r_ppb[q4 % 2][0]])
                op('dve', lambda e, q4=q4, pv=pv, n=n, c0=c0: e.tensor_tensor(
                    out=hT[:, 4 * q4:4 * q4 + 4, c0:c0 + n], in0=pv[:, 0:512].rearrange("p (a b) -> p a b", a=4)[:, :, :n],
                    in1=vec[gname][:, 4 * q4:4 * q4 + 4].unsqueeze(2).to_broadcast([128, 4, n]), op=ALU.mult),
                   r=[r_ppb[q4 % 2][0], r_c], w=[r_hT])

    def proj_fm(s, wbuf, wr, j, width, pidx):
        nco = NCOL[s]
        po = 1024 - nco
        for (c0, n) in st_chunks(s):
            for k in range(16):
                op('pe', lambda e, k=k, c0=c0, n=n: e.matmul(
                    pp[pidx][:, po + c0:po + c0 + n], lhsT=wbuf[:, k * width + 128 * j:k * width + 128 * (j + 1)],
                    rhs=hT[:, k, c0:c0 + n], start=(k == 0), stop=(k == 15)),
                   r=[wr, r_hT], w=[r_pp[pidx]])
        return pp[pidx][:, po:1024]

    y_rows = {}

    for s in range(CFG['nst']):
        nco = NCOL[s]
        po = 1024 - nco
        tiles = st_tiles(s)
        PEL = 'dve' if s == 0 else 'pool'
        pc0 = 32 if s == 0 else 0
        for (slot, c0, n) in tiles:
            if s == 0 and slot == 0:
                op(PEL, lambda e: e.memset(xres[:, 0, :], 0.0), w=[r_x[0]])
                P.dma('sp', xres[0:16, 0, :], I["xs"][:, :], w=[r_x[0]])
                P.dma('sp', xres[32:48, 0, :], I["meta"][:, :], w=[r_x[0]])
            else:
                gi = 4 * s + (slot - 1 if s == 0 else slot)
                y_rows[(s, slot)] = gi
                P.dma('sp', xres[:, slot, :], I["xp"][128 * gi:128 * (gi + 1), :], w=[r_x[slot]])
        P.dma('sp', TI[:, :nco], I["posc"][s:s + 1, 0:nco].partition_broadcast(128), w=[r_TI])
        sincos(TI[:, :nco], RINV_AP, nco, rcos[:, :nco], rsin[:, :nco], [r_TI], [r_rope], (0, 1))
        norm_to_hT(s, "g_mix")
        tap('hT%d' % s, hT[:, :, :nco], [r_hT])
        tap('rcos%d' % s, rcos[:, :nco], [r_rope])
        if CFG['stage'] < 2:
            continue
        tasks = [(pi, j) for pi in range(9) for j in range(2)]

        def do_proj(t):
            pi, j = tasks[t]
            wbuf, wr = mget(s, pi)
            if s == 0 and j == 0:
                bg_precast(2)
            return proj_fm(s, wbuf, wr, j, 256, j)

        pz_next = do_proj(0)
        for t in range(len(tasks)):
            pz = pz_next
            if t + 1 < len(tasks):
                pz_next = do_proj(t + 1)
            pi, j = tasks[t]
            if pi < 4:
                c = 2 * pi + j
                op('act', lambda e, c=c, pz=pz: e.activation(out=uT[:, c, :nco], in_=pz, func=AF.Copy),
                   r=[r_pp[j]], w=[r_u[c]])
            else:
                isq = pi < 8
                c = 2 * (pi - 4) + j if isq else j
                gv = vec["qg"] if isq else vec["kg"]
                op('act', lambda e, pz=pz: e.activation(out=sqb[:, :nco], in_=pz, func=AF.Square), r=[r_pp[j]], w=[r_sqb])
                for (c0, n) in st_chunks(s):
                    op('pe', lambda e, c0=c0, n=n: e.matmul(pp[2][:, po + c0:po + c0 + n], lhsT=cbf["blk64"][:],
                                                              rhs=sqb[:, c0:c0 + n], start=True, stop=True),
                       r=[r_sqb, r_c], w=[r_pp[2]])
                op('act', lambda e: e.activation(out=scr[2][:, :nco], in_=pp[2][:, po:1024], func=AF.Sqrt, scale=1.0 / 64,
                                                  bias=EPS_AP), r=[r_pp[2], r_small], w=[r_s[2]])
                op('dve', lambda e: e.reciprocal(out=scr[2][:, :nco], in_=scr[2][:, :nco]), r=[r_s[2]], w=[r_s[2]])
                op('dve', lambda e, pz=pz, gv=gv: e.scalar_tensor_tensor(out=scr[3][:, :nco], in0=pz, scalar=gv[:, 0:1],
                                                                        in1=scr[2][:, :nco], op0=ALU.mult, op1=ALU.mult),
                   r=[r_pp[j], r_s[2], r_c], w=[r_s[3]])
                op('act', lambda e: e.activation(out=sqb[:, :nco], in_=scr[3][:, :nco], func=AF.Copy), r=[r_s[3]], w=[r_sqb])
                for (c0, n) in st_chunks(s):
                    op('pe', lambda e, c0=c0, n=n: e.matmul(pp[2][:, po + c0:po + c0 + n], lhsT=cbf["rotm"][:],
                                                              rhs=sqb[:, c0:c0 + n], start=True, stop=True),
                       r=[r_sqb, r_c], w=[r_pp[2]])
                op(PEL, lambda e: e.tensor_tensor(out=scr[3][:, :nco], in0=scr[3][:, :nco], in1=rcos[:, :nco], op=ALU.mult),
                   r=[r_s[3], r_rope], w=[r_s[3]])
                op('dve', lambda e: e.tensor_tensor(out=scr[2][:, :nco], in0=pp[2][:, po:1024], in1=rsin[:, :nco], op=ALU.mult),
                   r=[r_pp[2], r_rope], w=[r_s[2]])
                if isq:
                    op(PEL, lambda e, c=c: e.tensor_tensor(out=QT[:, c, :nco], in0=scr[3][:, :nco], in1=scr[2][:, :nco],
                                                               op=ALU.add), r=[r_s[3], r_s[2]], w=[r_Q])
                else:
                    op(PEL, lambda e: e.tensor_tensor(out=scr[3][:, :nco], in0=scr[3][:, :nco], in1=scr[2][:, :nco],
                                                          op=ALU.add), r=[r_s[3], r_s[2]], w=[r_s[3]])
                    op('act', lambda e, c=c: e.activation(out=KT[:, c, 128:128 + nco], in_=scr[3][:, :nco], func=AF.Copy),
                       r=[r_s[3]], w=[r_K])
                    if s == 0:
                        op('pe', lambda e: e.transpose(pp[3][0:16, 0:128], scr[3][:, 0:16], cst["ident"][:]),
                           r=[r_s[3], r_c], w=[r_pp[3]])
                        op('dve', lambda e, c=c: e.tensor_copy(out=outst[0:16, 128 * c:128 * (c + 1)], in_=pp[3][0:16, 0:128]),
                           r=[r_pp[3]], w=[r_outst])
                    if s == 3:
                        op('pe', lambda e: e.transpose(pp[3][:, 0:128], scr[3][:, 384:512], cst["ident"][:]),
                           r=[r_s[3], r_c], w=[r_pp[3]])
                        op('dve', lambda e, c=c: e.tensor_copy(out=outst[:, 128 * c:128 * (c + 1)], in_=pp[3][:, 0:128]),
                           r=[r_pp[3]], w=[r_outst])
            if pi == 8 and j == 1:
                if s == 0 and not CFG.get('nocache') and not CFG.get('nok'):
                    P.dma('sp', O["sck"][:, 127, :], outst[0:16, 0:256], r=[r_outst], w=[r_sck])
                    if not CFG.get('nod2d'):
                        P.dma('sp', O["sck"][:, 0:127, :], I["ck"][:, 1:128, :], r=[r_sck], w=[r_sck])
                if s == 3:
                    P.dma('sp', O["pck"][:, :], outst[:, 0:256], r=[r_outst])

        wbuf, wr = mget(s, 9)
        if s == 0:
            bg_precast(2)
        for (slot, c0, n) in tiles:
            pidx = slot % 2
            for k in range(16):
                op('pe', lambda e, k=k, c0=c0, n=n, pidx=pidx: e.matmul(
                    pp[pidx][:n, 0:256], lhsT=hT[:, k, c0:c0 + n], rhs=wbuf[:, k * 256:(k + 1) * 256],
                    start=(k == 0), stop=(k == 15)), r=[wr, r_hT], w=[r_pp[pidx]])
            vs = slot + 1 if s > 0 else slot + 1
            op('act', lambda e, n=n, pidx=pidx, vs=vs: e.activation(
                out=V1[:n, vs, :, 0:64], in_=pp[pidx][:n, 0:256].rearrange("p (a b) -> p a b", a=4), func=AF.Copy),
               r=[r_pp[pidx]], w=[r_V[vs]])
            if s == 0 and slot == 0 and not CFG.get('nocache') and not CFG.get('nov'):
                op('dve', lambda e, pidx=pidx: e.tensor_copy(out=outst[0:16, 256:512], in_=pp[pidx][0:16, 0:256]),
                   r=[r_pp[pidx]], w=[r_outst])
                if not CFG.get('novdma'):
                    P.dma('sp', O["scv"][:, 127, :], outst[0:16, 256:512], r=[r_outst], w=[r_scv])
                if not CFG.get('nod2d'):
                    P.dma('sp', O["scv"][:, 0:127, :], I["cv"][:, 1:128, :], r=[r_scv], w=[r_scv])
            if s == 3 and slot == 3:
                op('dve', lambda e, pidx=pidx: e.tensor_copy(out=outst[:, 256:512], in_=pp[pidx][:, 0:256]),
                   r=[r_pp[pidx]], w=[r_outst])
                P.dma('sp', O["pcv"][:, :], outst[:, 256:512], r=[r_outst])

        tap('uT%d' % s, uT[:, :, :nco], r_u)
        tap('QT%d' % s, QT[:, :, :nco], [r_Q])
        tap('KT%d' % s, KT[:, :, :], [r_K])
        tap('V1%d' % s, V1[:, :, :, :], r_V)
        if CFG['stage'] < 3:
            continue
        npc = nco - pc0

        def ssm_pro(c):
            for (c0, n) in st_chunks(s):
                op('pe', lambda e, c=c, c0=c0, n=n: e.matmul(pp[3][:, po + c0:po + c0 + n], lhsT=Dd[:, c, :], rhs=uT[:, c, c0:c0 + n],
                                                              start=True, stop=False), r=[r_ssmc, r_u[c]], w=[r_pp[3]])
            if s == 0:
                P.dma('sp', h0tok[:, :, 0:64], I["sre"][:, 8 * c:8 * c + 8, :], w=[r_h0])
                P.dma('sp', h0tok[:, :, 64:128], I["sim"][:, 8 * c:8 * c + 8, :], w=[r_h0])
                op('dve', lambda e: e.tensor_scalar(out=h0tok2[:, :, 0:64], in0=h0tok[:, :, 64:128], scalar1=-1.0, scalar2=None,
                                                     op0=ALU.mult), r=[r_h0], w=[r_h0])
                op('dve', lambda e: e.tensor_copy(out=h0tok2[:, :, 64:128], in_=h0tok[:, :, 0:64]), r=[r_h0], w=[r_h0])
                for gi in range(8):
                    op('pe', lambda e, gi=gi: e.transpose(pp[2][:, 16 * gi:16 * gi + 16], h0tok[:, gi, :], cst["ident"][0:16, 0:16]),
                       r=[r_h0, r_c], w=[r_pp[2]])
                    op('pe', lambda e, gi=gi: e.transpose(pp[2][:, 128 + 16 * gi:128 + 16 * gi + 16], h0tok2[:, gi, :],
                                                         cst["ident"][0:16, 0:16]), r=[r_h0, r_c], w=[r_pp[2]])
                op('act', lambda e: e.activation(out=h0T[:], in_=pp[2][:, 0:128].rearrange("p (a b) -> p a b", a=8), func=AF.Copy),
                   r=[r_pp[2]], w=[r_h0])
                op('act', lambda e: e.activation(out=h0T2[:], in_=pp[2][:, 128:256].rearrange("p (a b) -> p a b", a=8), func=AF.Copy),
                   r=[r_pp[2]], w=[r_h0])

        def ssm_tabA(g):
            c, gi = divmod(g, 8)
            g = 8 * c + gi
            j, s_ = gi // 2, gi % 2
            cs = 2 * c + s_
            par = g % 2
            iA, iS, iC, iZ, iY = [5 * par + q for q in range(5)]
            bA, bS, bC, bZ, bY = scr[iA], scr[iS], scr[iC], scr[iZ], scr[iY]
            W1p, W2p, rW1, rW2 = W1x[par], W2x[par], r_W1x[par], r_W2x[par]
            tis = TI[:, pc0:nco]
            thg = T["th"][:, g:g + 1]
            op('act', lambda e: e.activation(out=bA[:, :npc], in_=tis, func=AF.Identity, scale=thg, bias=M_AP),
               r=[r_TI, r_small, r_ssmc], w=[r_s[iA]])
            op('act', lambda e: e.activation(out=bA[:, :npc], in_=bA[:, :npc], func=AF.Identity, bias=NM_AP),
               r=[r_s[iA], r_small], w=[r_s[iA]])
            op('dve', lambda e: e.scalar_tensor_tensor(out=bA[:, :npc], in0=tis, scalar=thg, in1=bA[:, :npc], op0=ALU.mult,
                                                        op1=ALU.subtract), r=[r_TI, r_ssmc, r_s[iA]], w=[r_s[iA]])

        def ssm_tabB(g):
            c, gi = divmod(g, 8)
            g = 8 * c + gi
            j, s_ = gi // 2, gi % 2
            cs = 2 * c + s_
            par = g % 2
            iA, iS, iC, iZ, iY = [5 * par + q for q in range(5)]
            bA, bS, bC, bZ, bY = scr[iA], scr[iS], scr[iC], scr[iZ], scr[iY]
            W1p, W2p, rW1, rW2 = W1x[par], W2x[par], r_W1x[par], r_W2x[par]
            tis = TI[:, pc0:nco]
            thg = T["th"][:, g:g + 1]
            op('act', lambda e: e.activation(out=bS[:, :npc], in_=bA[:, :npc], func=AF.Sin, scale=TWO_PI), r=[r_s[iA]], w=[r_s[iS]])
            op('act', lambda e: e.activation(out=bA[:, :npc], in_=bA[:, :npc], func=AF.Abs), r=[r_s[iA]], w=[r_s[iA]])
            op('act', lambda e: e.activation(out=bC[:, :npc], in_=bA[:, :npc], func=AF.Sin, scale=-TWO_PI, bias=HPI_AP),
               r=[r_s[iA], r_small], w=[r_s[iC]])

        def ssm_xaxb(g):
            c, gi = divmod(g, 8)
            g = 8 * c + gi
            j, s_ = gi // 2, gi % 2
            cs = 2 * c + s_
            par = g % 2
            iA, iS, iC, iZ, iY = [5 * par + q for q in range(5)]
            bA, bS, bC, bZ, bY = scr[iA], scr[iS], scr[iC], scr[iZ], scr[iY]
            W1p, W2p, rW1, rW2 = W1x[par], W2x[par], r_W1x[par], r_W2x[par]
            tis = TI[:, pc0:nco]
            thg = T["th"][:, g:g + 1]
            for v in range(2):
                for (c0, n) in st_chunks(s):
                    op('pe', lambda e, v=v, c0=c0, n=n, j=j, cs=cs, c=c: e.matmul(
                        pp[v][:, po + c0:po + c0 + n], lhsT=Bst[v][32 * j:32 * j + 32, cs, :],
                        rhs=uT[32 * j:32 * j + 32, c, c0:c0 + n], start=True, stop=True, tile_position=(32 * j, 0)),
                       r=[r_ssmc, r_u[c]], w=[r_pp[v]])

        def ssm_main(g):
            c, gi = divmod(g, 8)
            g = 8 * c + gi
            j, s_ = gi // 2, gi % 2
            cs = 2 * c + s_
            par = g % 2
            iA, iS, iC, iZ, iY = [5 * par + q for q in range(5)]
            bA, bS, bC, bZ, bY = scr[iA], scr[iS], scr[iC], scr[iZ], scr[iY]
            W1p, W2p, rW1, rW2 = W1x[par], W2x[par], r_W1x[par], r_W2x[par]
            tis = TI[:, pc0:nco]
            thg = T["th"][:, g:g + 1]
            op('dve', lambda e: e.tensor_tensor(out=bZ[:, :npc], in0=pp[0][:, po + pc0:1024], in1=bC[:, :npc], op=ALU.mult),
               r=[r_pp[0], r_s[iC]], w=[r_s[iZ]])
            op('dve', lambda e: e.tensor_tensor(out=bY[:, :npc], in0=pp[1][:, po + pc0:1024], in1=bS[:, :npc], op=ALU.mult),
               r=[r_pp[1], r_s[iS]], w=[r_s[iY]])
            if s == 0:
                op('dve', lambda e, g=g, gi=gi: e.scalar_tensor_tensor(out=sT[:, gi, :], in0=h0T[:, gi, :], scalar=T["ar"][:, g:g + 1],
                                                                      in1=pp[0][:, po:po + 16], op0=ALU.mult, op1=ALU.add),
                   r=[r_h0, r_ssmc, r_pp[0]], w=[r_h0])
                op('dve', lambda e, g=g, gi=gi: e.scalar_tensor_tensor(out=sT[:, gi, :], in0=h0T2[:, gi, :], scalar=T["ai"][:, g:g + 1],
                                                                      in1=sT[:, gi, :], op0=ALU.mult, op1=ALU.add),
                   r=[r_h0, r_ssmc], w=[r_h0])
            if g + 1 < 64:
                ssm_xaxb(g + 1)
            op('dve', lambda e: e.tensor_tensor(out=bZ[:, :npc], in0=bZ[:, :npc], in1=bY[:, :npc], op=ALU.add),
               r=[r_s[iZ], r_s[iY]], w=[r_s[iZ]])
            op('dve', lambda e, g=g: e.tensor_tensor_scan(out=bY[:, :npc], data0=T["mag"][:, g:g + 1].to_broadcast([128, npc]),
                                                          data1=bZ[:, :npc], initial=T["zl"][:, g:g + 1], op0=ALU.mult, op1=ALU.add),
               r=[r_s[iZ], r_ssmc, r_zl], w=[r_s[iY]])
            op('dve', lambda e, g=g: e.tensor_copy(out=T["zl"][:, g:g + 1], in_=bY[:, npc - 1:npc]),
               r=[r_s[iY]], w=[r_zl])
            op(PEL, lambda e: e.tensor_tensor(out=W1p[:, pc0:nco], in0=bY[:, :npc], in1=bC[:, :npc], op=ALU.mult),
               r=[r_s[iY], r_s[iC]], w=[rW1])
            op(PEL, lambda e: e.tensor_tensor(out=W2p[:, pc0:nco], in0=bY[:, :npc], in1=bS[:, :npc], op=ALU.mult),
               r=[r_s[iY], r_s[iS]], w=[rW2])
            if s == 3:
                op('dve', lambda e, g=g: e.tensor_tensor(out=fin[:, g:g + 1], in0=bY[:, npc - 1:npc], in1=bC[:, npc - 1:npc], op=ALU.mult),
                   r=[r_s[iY], r_s[iC], r_fin], w=[r_fin])
                op('dve', lambda e, g=g: e.tensor_tensor(out=fin[:, 64 + g:65 + g], in0=bY[:, npc - 1:npc], in1=bS[:, npc - 1:npc], op=ALU.mult),
                   r=[r_s[iY], r_s[iS], r_fin], w=[r_fin])
            if s == 0:
                op('dve', lambda e, gi=gi: e.tensor_copy(out=W1p[:, 0:16], in_=sT[:, gi, :]), r=[r_h0], w=[rW1])
            for v, Wx, rW in ((0, W1p, rW1), (1, W2p, rW2)):
                for (c0, n) in st_chunks(s):
                    last = (gi == 7 and v == 1)
                    op('pe', lambda e, v=v, Wx=Wx, c0=c0, n=n, j=j, g=g, last=last: e.matmul(
                        pp[3][32 * j:32 * j + 32, po + c0:po + c0 + n], lhsT=Cst[v][:, g, :], rhs=Wx[:, c0:c0 + n],
                        start=False, stop=last, tile_position=(0, 32 * j)), r=[r_ssmc, rW], w=[r_pp[3]])

        def ssm_epi(c):
            if s == 0:
                for gi in range(8):
                    op('pe', lambda e, gi=gi: e.transpose(pp[2][0:16, 128 * gi:128 * (gi + 1)], sT[:, gi, :], cst["ident"][:]),
                       r=[r_h0, r_c], w=[r_pp[2]])
                op('act', lambda e: e.activation(out=h0tok, in_=pp[2][0:16, 0:1024].rearrange("p (a b) -> p a b", a=8), func=AF.Copy),
                   r=[r_pp[2]], w=[r_h0])
                P.dma('sp', O["ssre"][:, 8 * c:8 * c + 8, :], h0tok[:, :, 0:64], r=[r_h0])
                P.dma('sp', O["ssim"][:, 8 * c:8 * c + 8, :], h0tok[:, :, 64:128], r=[r_h0])
            yv = pp[3][:, po:1024]
            op('act', lambda e, yv=yv: e.activation(out=scr[6][:, :nco], in_=yv, func=AF.Square), r=[r_pp[3]], w=[r_s[6]])
            op('dve', lambda e: e.tensor_scalar(out=scr[6][:, :nco], in0=scr[6][:, :nco], scalar1=0.044715, scalar2=1.0, op0=ALU.mult,
                                                 op1=ALU.add), r=[r_s[6]], w=[r_s[6]])
            op('dve', lambda e, yv=yv: e.tensor_tensor(out=scr[6][:, :nco], in0=yv, in1=scr[6][:, :nco], op=ALU.mult),
               r=[r_pp[3], r_s[6]], w=[r_s[6]])
            op('act', lambda e: e.activation(out=scr[6][:, :nco], in_=scr[6][:, :nco], func=AF.Sigmoid, scale=1.5957691216),
               r=[r_s[6]], w=[r_s[6]])
            op('dve', lambda e, yv=yv, c=c: e.tensor_tensor(out=uT[:, c, :nco], in0=yv, in1=scr[6][:, :nco], op=ALU.mult),
               r=[r_pp[3], r_s[6]], w=[r_u[c]])
        tap('yg%d' % s, uT[:, :, :nco], r_u)
        if CFG['stage'] < 4:
            continue

        if s == 0:
            for par_ in range(2):
                op('dve', lambda e, par_=par_: e.memset(W1x[par_][:, 16:32], 0.0), w=[r_W1x[par_]])
                op('dve', lambda e, par_=par_: e.memset(W2x[par_][:, 0:32], 0.0), w=[r_W2x[par_]])
        ssm_tabA(0)
        ssm_tabA(1)
        ssm_tabB(0)
        ssm_xaxb(0)
        for g in range(64):
            if g + 2 < 64:
                ssm_tabA(g + 2)
            if g + 1 < 64:
                ssm_tabB(g + 1)
            if g % 8 == 0:
                ssm_pro(g // 8)
            if s == 0 and g % 4 == 0:
                bg_precast(3)
            ssm_main(g)
            if g % 8 == 7:
                ssm_epi(g // 8)
        if s == 3:
            op('pe', lambda e: e.transpose(pp[2][0:64, 0:128], fin[:, 0:64], cst["ident"][:]), r=[r_fin, r_c], w=[r_pp[2]])
            op('pe', lambda e: e.transpose(pp[2][0:64, 128:256], fin[:, 64:128], cst["ident"][:]), r=[r_fin, r_c], w=[r_pp[2]])
            op('act', lambda e: e.activation(out=outst[0:64, 256:512], in_=pp[2][0:64, 0:256], func=AF.Copy), r=[r_pp[2]], w=[r_outst])
            op('dve', lambda e: e.tensor_tensor(out=outst[0:64, 0:64], in0=outst[0:64, 256:320], in1=outst[0:64, 448:512], op=ALU.subtract),
               r=[r_outst], w=[r_outst])
            op('dve', lambda e: e.tensor_tensor(out=outst[0:64, 64:128], in0=outst[0:64, 384:448], in1=outst[0:64, 320:384], op=ALU.add),
               r=[r_outst], w=[r_outst])
            P.dma('sp', O["pre"][:, :], outst[0:64, 0:64], r=[r_outst])
            P.dma('sp', O["pim"][:, :], outst[0:64, 64:128], r=[r_outst])
        wg = I["w_glu"].rearrange("(k p) n -> p k n", p=128)
        mixT = hT
        for half in range(2):
            bufs = []
            for q in range(2):
                bufs.append(None)
            for q in range(2):
                wbuf, wr = mget(s, 10 + 2 * half + q)
                for j in range(2):
                    oc = 4 * half + 2 * q + j
                    for (c0, n) in st_chunks(s):
                        for k in range(8):
                            op('pe', lambda e, k=k, c0=c0, n=n, j=j, wbuf=wbuf: e.matmul(
                                pp[j][:, po + c0:po + c0 + n], lhsT=wbuf[:, k * 256 + 128 * j:k * 256 + 128 * (j + 1)],
                                rhs=uT[:, k, c0:c0 + n], start=(k == 0), stop=(k == 7)), r=[wr] + r_u, w=[r_pp[j]])
                    op('act', lambda e, j=j, oc=oc: e.activation(out=scr[2][:, :nco], in_=pp[j][:, po:1024], func=AF.Sigmoid,
                                                                  bias=vec["b_glu"][:, oc:oc + 1]), r=[r_pp[j], r_c], w=[r_s[2]])
                    op('dve', lambda e, oc=oc: e.tensor_tensor(out=scr[2][:, :nco], in0=scr[2][:, :nco], in1=uT[:, oc, :nco], op=ALU.mult),
                       r=[r_s[2]] + r_u, w=[r_s[2]])
                    op('act', lambda e: e.activation(out=sqb[:, :nco], in_=scr[2][:, :nco], func=AF.Square), r=[r_s[2]], w=[r_sqb])
                    for (c0, n) in st_chunks(s):
                        op('pe', lambda e, c0=c0, n=n, oc=oc: e.matmul(pp[2][:, po + c0:po + c0 + n], lhsT=cbf["ones"][:],
                                                                        rhs=sqb[:, c0:c0 + n], start=(oc == 0), stop=(oc == 7)),
                           r=[r_sqb, r_c], w=[r_pp[2]])
                    op('dve', lambda e, oc=oc: e.tensor_scalar(out=mixT[:, oc, :nco], in0=scr[2][:, :nco], scalar1=vec["g_s"][:, oc:oc + 1],
                                                                scalar2=None, op0=ALU.mult), r=[r_s[2], r_c], w=[r_hT])
        op('act', lambda e: e.activation(out=scr[2][:, :nco], in_=pp[2][:, po:1024], func=AF.Sqrt, scale=1.0 / 1024, bias=EPS_AP),
           r=[r_pp[2], r_small], w=[r_s[2]])
        op('dve', lambda e: e.reciprocal(out=scr[2][:, :nco], in_=scr[2][:, :nco]), r=[r_s[2]], w=[r_s[2]])
        for oc in range(8):
            op('dve', lambda e, oc=oc: e.tensor_tensor(out=mixT[:, oc, :nco], in0=mixT[:, oc, :nco], in1=scr[2][:, :nco], op=ALU.mult),
               r=[r_s[2], r_hT], w=[r_hT])

        tap('mixs%d' % s, hT[:, 0:8, :nco], [r_hT])
        if CFG['stage'] < 5:
            continue
        def att_finish(n, c0):
            op('dve', lambda e: e.memset(small[:n, 2:3], 0.0), r=[r_small], w=[r_small])
            op('act', lambda e: e.activation(out=sqb[:n, 0:1024], in_=att[:n].rearrange("p a b -> p (a b)"), func=AF.Square,
                                              accum_out=small[:n, 2:3]), r=[r_att, r_small], w=[r_sqb, r_small])
            op('act', lambda e: e.activation(out=small[:n, 3:4], in_=small[:n, 2:3], func=AF.Sqrt, scale=1.0 / 1024, bias=EPS_AP[:n]),
               r=[r_small], w=[r_small])
            op('dve', lambda e: e.reciprocal(out=small[:n, 3:4], in_=small[:n, 3:4]), r=[r_small], w=[r_small])
            op('dve', lambda e: e.tensor_scalar(out=attn[:n, :], in0=att[:n].rearrange("p a b -> p (a b)"), scalar1=small[:n, 3:4],
                                                 scalar2=None, op0=ALU.mult), r=[r_att, r_small], w=[r_attn])
            pv = pbf(2)
            for jj in range(8):
                op('pe', lambda e, jj=jj: e.transpose(pv[:, 128 * jj:128 * jj + n], attn[:n, 128 * jj:128 * (jj + 1)], cbf["ident"][:n, :n]),
                   r=[r_attn, r_c], w=[r_pp[2]])
            for jj in range(8):
                op('dve', lambda e, jj=jj: e.tensor_scalar(out=mixT[:, 8 + jj, c0:c0 + n], in0=pv[:, 128 * jj:128 * jj + n],
                                                            scalar1=vec["g_a"][:, jj:jj + 1], scalar2=None, op0=ALU.mult),
                   r=[r_pp[2], r_c], w=[r_hT])

        def att_norm_kvh(n, kvh, ovw):
            op('dve', lambda e: e.tensor_tensor(out=small[:n, 8:12], in0=ovw[:, :, 64], in1=esink[:n, 4 * kvh:4 * kvh + 4], op=ALU.add),
               r=[r_pp[3], r_c, r_small], w=[r_small])
            op('dve', lambda e: e.reciprocal(out=small[:n, 8:12], in_=small[:n, 8:12]), r=[r_small], w=[r_small])
            op('dve', lambda e: e.tensor_tensor(out=att[:n, 4 * kvh:4 * kvh + 4, :], in0=ovw[:, :, 0:64],
                                                 in1=small[:n, 8:12].unsqueeze(2).to_broadcast([n, 4, 64]), op=ALU.mult),
               r=[r_pp[3], r_small], w=[r_att])

        for (slot, c0, n) in tiles:
            special = (s == 0 and slot == 0)
            if special:
                prev = None
                mc = cbf["m48c"]
            else:
                mc = cbf["maskc"]
                if s == 0 and slot == 1:
                    prev = (0, 48, cbf["m48p"], 1)
                else:
                    prev = (c0 - 128, 128, cbf["maskp"], slot if s > 0 else slot)
            vcur = slot + 1
            for kvh in range(4):
                pb = 64 * (kvh % 2)
                kc = kvh // 2
                q0 = 4 * (kvh // 2)
                ps_ = pp[kvh % 2]
                rps = r_pp[kvh % 2]
                op('pe', lambda e, ps_=ps_, pb=pb, kc=kc, q0=q0: e.matmul(
                    ps_[:n, 0:4 * n].rearrange("p (a b) -> p a b", a=4), lhsT=KT[pb:pb + 64, kc, 128 + c0:128 + c0 + n],
                    rhs=QT[pb:pb + 64, q0:q0 + 4, c0:c0 + n], start=True, stop=True, tile_position=(pb, 0)),
                   r=[r_K, r_Q], w=[rps])
                op('act', lambda e, ps_=ps_: e.activation(out=PTc[:n, :, :n], in_=ps_[:n, 0:4 * n].rearrange("p (a b) -> p a b", a=4),
                                                          func=AF.Exp, scale=0.125), r=[rps], w=[r_PTc])
                op(PEL, lambda e, mc=mc: e.tensor_tensor(out=PTc[:n, :, :n], in0=PTc[:n, :, :n],
                                                             in1=mc[:n, :n].unsqueeze(1).to_broadcast([n, 4, n]), op=ALU.mult),
                   r=[r_PTc, r_c], w=[r_PTc])
                if prev is not None:
                    pc, nk, mp, vsl = prev
                    op('pe', lambda e, ps_=ps_, pb=pb, kc=kc, q0=q0, pc=pc, nk=nk: e.matmul(
                        ps_[:nk, 512:512 + 4 * n].rearrange("p (a b) -> p a b", a=4), lhsT=KT[pb:pb + 64, kc, 128 + pc:128 + pc + nk],
                        rhs=QT[pb:pb + 64, q0:q0 + 4, c0:c0 + n], start=True, stop=True, tile_position=(pb, 0)),
                       r=[r_K, r_Q], w=[rps])
                    op('act', lambda e, ps_=ps_, nk=nk: e.activation(out=PTp[:nk, :, :n],
                                                                     in_=ps_[:nk, 512:512 + 4 * n].rearrange("p (a b) -> p a b", a=4),
                                                                     func=AF.Exp, scale=0.125), r=[rps], w=[r_PTp])
                    op(PEL, lambda e, mp=mp, nk=nk: e.tensor_tensor(out=PTp[:nk, :, :n], in0=PTp[:nk, :, :n],
                                                                        in1=mp[:nk, :n].unsqueeze(1).to_broadcast([nk, 4, n]), op=ALU.mult),
                       r=[r_PTp, r_c], w=[r_PTp])
                ov = pp[3][:n, 0:260].rearrange("p (a b) -> p a b", a=4)
                for g4 in range(4):
                    if prev is not None:
                        pc, nk, mp, vsl = prev
                        op('pe', lambda e, g4=g4, nk=nk, vsl=vsl, kvh=kvh: e.matmul(pp[3][:n, 65 * g4:65 * g4 + 65], lhsT=PTp[:nk, g4, :n],
                                                                                  rhs=V1[:nk, vsl, kvh, :], start=True, stop=False),
                           r=[r_PTp, r_V[vsl]], w=[r_pp[3]])
                    op('pe', lambda e, g4=g4, kvh=kvh: e.matmul(pp[3][:n, 65 * g4:65 * g4 + 65], lhsT=PTc[:n, g4, :n],
                                                              rhs=V1[:n, vcur, kvh, :], start=(prev is None), stop=True),
                       r=[r_PTc, r_V[vcur]], w=[r_pp[3]])
                att_norm_kvh(n, kvh, ov)
            if special:
                for hb in range(2):
                    P.dma('pool', Kb, O["sck"][8 * hb:8 * hb + 8, :, :].rearrange("b k f -> k b f"), r=[r_sck], w=[r_Kb])
                    for a4 in range(4):
                        P.dma('pool', Vb[:, :, a4, 0:64], O["scv"][8 * hb:8 * hb + 8, :, 64 * a4:64 * a4 + 64].rearrange("b k d -> k b d"),
                              r=[r_scv], w=[r_Vb])
                    for bb in range(8):
                        b = 8 * hb + bb
                        pv = pbf(2)
                        for kc in range(2):
                            op('pe', lambda e, bb=bb, kc=kc: e.transpose(pv[:, 128 * kc:128 * (kc + 1)], Kb[:, bb, 128 * kc:128 * (kc + 1)],
                                                                        cbf["ident"][:]), r=[r_Kb, r_c], w=[r_pp[2]])
                        op('act', lambda e: e.activation(out=KbT, in_=pv[:, 0:256].rearrange("p (a b) -> p a b", a=2), func=AF.Copy),
                           r=[r_pp[2]], w=[r_KbT])
                        for kvh in range(4):
                            pb = 64 * (kvh % 2)
                            kc = kvh // 2
                            q0 = 4 * (kvh // 2)
                            op('pe', lambda e, b=b, pb=pb, kc=kc, q0=q0, kvh=kvh: e.matmul(
                                pp[0][:, 16 * b + 4 * kvh:16 * b + 4 * kvh + 4], lhsT=KbT[pb:pb + 64, kc, :],
                                rhs=QT[pb:pb + 64, q0:q0 + 4, b], start=True, stop=True, tile_position=(pb, 0)),
                               r=[r_KbT, r_Q], w=[r_pp[0]])
                    op('act', lambda e, hb=hb: e.activation(out=PTs[:, 128 * hb:128 * hb + 128], in_=pp[0][:, 128 * hb:128 * hb + 128],
                                                            func=AF.Exp, scale=0.125), r=[r_pp[0]], w=[r_PTs])
                    for bb in range(8):
                        b = 8 * hb + bb
                        for kvh in range(4):
                            op('pe', lambda e, b=b, bb=bb, kvh=kvh: e.matmul(pp[1][0:65, 16 * b + 4 * kvh:16 * b + 4 * kvh + 4],
                                                                           lhsT=Vb[:, bb, kvh, :], rhs=PTs[:, 16 * b + 4 * kvh:16 * b + 4 * kvh + 4],
                                                                           start=True, stop=True), r=[r_Vb, r_PTs], w=[r_pp[1]])
                op('act', lambda e: e.activation(out=OTs, in_=pp[1][0:65, 0:256], func=AF.Copy), r=[r_pp[1]], w=[r_OTs])
                for kvh in range(4):
                    for g4 in range(4):
                        h = 4 * kvh + g4
                        op('pe', lambda e, h=h, g4=g4: e.transpose(pp[3][0:16, 65 * g4:65 * g4 + 65],
                                                                  OTs.rearrange("p (b h) -> p h b", h=16)[:, h, :],
                                                                  cst["ident"][0:65, 0:65]), r=[r_OTs, r_c], w=[r_pp[3]])
                    att_norm_kvh(16, kvh, pp[3][0:16, 0:260].rearrange("p (a b) -> p a b", a=4))
            att_finish(n, c0)
        if s < 3:
            op('act', lambda e: e.activation(out=KT[:, :, 0:128], in_=KT[:, :, nco:nco + 128], func=AF.Copy), r=[r_K], w=[r_K])
            lastv = tiles[-1][0] + 1
            op('act', lambda e: e.activation(out=V1[:, 0 if s > 0 else 0], in_=V1[:, lastv], func=AF.Copy), r=[r_V[lastv]], w=[r_V[0]])

        tap('mixa%d' % s, hT[:, 8:16, :nco], [r_hT])
        if CFG['stage'] < 6:
            continue
        wo = I["w_out"].rearrange("(k p) n -> p k n", p=128)
        for pi in range(8):
            wbuf, wr = mget(s, 14 + pi)
            for (slot, c0, n) in tiles:
                pidx = slot % 2
                for k in range(16):
                    op('pe', lambda e, k=k, c0=c0, n=n, pidx=pidx: e.matmul(pp[pidx][:n, 0:256], lhsT=mixT[:, k, c0:c0 + n],
                                                                          rhs=wbuf[:, k * 256:(k + 1) * 256], start=(k == 0), stop=(k == 15)),
                       r=[wr, r_hT], w=[r_pp[pidx]])
                op('dve', lambda e, n=n, pidx=pidx, slot=slot, pi=pi: e.tensor_tensor(
                    out=xres[:n, slot, 256 * pi:256 * (pi + 1)], in0=pp[pidx][:n, 0:256], in1=xres[:n, slot, 256 * pi:256 * (pi + 1)],
                    op=ALU.add), r=[r_pp[pidx], r_x[slot]], w=[r_x[slot]])

        tap('xmid%d' % s, xres[:, :, :], r_x)
        if CFG['stage'] < 7:
            continue
        norm_to_hT(s, "g_ffn")
        wu = I["w_up"].rearrange("(k p) n -> p k n", p=128)
        wd = I["w_down"].rearrange("(k p) n -> p k n", p=128)
        BV = [(wb[0], r_wb[0]), (wb[1], r_wb[1])]
        BG = [(wb[2], r_wb[2]), (wb[3], r_wb[3])]
        BDx = [(wb[4], r_wb[4]), (wb[5], r_wb[5])]

        def load_up(g):
            wfill(1, BV[g % 2][0], BV[g % 2][1], sc_ffn[3 * g], ('ffn', 3 * g), 4096, None, 16)
            wfill(1, BG[g % 2][0], BG[g % 2][1], sc_ffn[3 * g + 1], ('ffn', 3 * g + 1), 4096, None, 16)

        def load_dn(g):
            wfill(1, BDx[g % 2][0], BDx[g % 2][1], sc_ffn[3 * g + 2], ('ffn', 3 * g + 2), 4096, None, 2)

        def ffn_up(g):
            bv, rv = BV[g % 2]
            bg, rg = BG[g % 2]
            actT, r_actT = actTx[g % 2], r_actTx[g % 2]
            if s == 0:
                scv_ = I["sconv"].rearrange("b j (v f) -> (b j) v f", v=2)[:, :, 256 * g:256 * (g + 1)]
                P.dma('act', tokst.rearrange("p (v f) -> p v f", v=2), scv_, w=[r_tokst])
                for q in range(4):
                    op('pe', lambda e, q=q: e.transpose(pp[3][:, 32 * q:32 * q + 32], tokst[:, 128 * q:128 * (q + 1)], cst["ident"][0:32, 0:32]),
                       r=[r_tokst, r_c], w=[r_ppb[3][0]])
                op('act', lambda e: e.activation(out=stT[:], in_=pp[3][:, 0:128].rearrange("p (a b) -> p a b", a=4), func=AF.Copy),
                   r=[r_ppb[3][0]], w=[r_stT])
            for j in range(2):
                c = 2 * g + j
                for vi, (wbuf, wr) in enumerate(((bv, rv), (bg, rg))):
                    cc = c + 44 * vi
                    pz = proj_fm(s, wbuf, wr, j, 256, vi)
                    ub = scr[vi]
                    cvb = scr[2 + vi]
                    rc = r_s[2 + vi]
                    op('act', lambda e, ub=ub, cc=cc: e.activation(out=ub[:, 0:2], in_=convst[:, cc, :], func=AF.Copy),
                       r=[r_convst], w=[r_s[vi]])
                    op('act', lambda e, ub=ub, pz=pz: e.activation(out=ub[:, 2:2 + nco], in_=pz, func=AF.Copy), r=[r_pp[vi]], w=[r_s[vi]])
                    op('act', lambda e, cvb=cvb, pz=pz, cc=cc: e.activation(out=cvb[:, :nco], in_=pz, func=AF.Identity,
                                                                         scale=vec["cw"][:, 2, cc:cc + 1], bias=vec["cb"][:, cc:cc + 1]),
                       r=[r_pp[vi], r_c], w=[rc])
                    op('act', lambda e, ub=ub, cc=cc: e.activation(out=convst[:, cc, :], in_=ub[:, nco:nco + 2], func=AF.Copy),
                       r=[r_s[vi]], w=[r_convst])
                    op('dve', lambda e, ub=ub, cvb=cvb, cc=cc: e.scalar_tensor_tensor(out=cvb[:, :nco], in0=ub[:, 1:1 + nco], scalar=vec["cw"][:, 1, cc:cc + 1],
                                                                                     in1=cvb[:, :nco], op0=ALU.mult, op1=ALU.add),
                       r=[r_s[vi], r_c, rc], w=[rc])
                    op('dve', lambda e, ub=ub, cvb=cvb, cc=cc: e.scalar_tensor_tensor(out=cvb[:, :nco], in0=ub[:, 0:nco], scalar=vec["cw"][:, 0, cc:cc + 1],
                                                                                     in1=cvb[:, :nco], op0=ALU.mult, op1=ALU.add),
                       r=[r_s[vi], r_c, rc], w=[rc])
                    if s == 0:
                        q = 2 * vi + j
                        st3 = stT[:, q, :].rearrange("p (b t) -> p b t", t=2)
                        op('dve', lambda e, ub=ub, cvb=cvb, cc=cc: e.tensor_scalar(out=cvb[:, 0:16], in0=ub[:, 2:18], scalar1=vec["cw"][:, 2, cc:cc + 1],
                                                                                  scalar2=vec["cb"][:, cc:cc + 1], op0=ALU.mult, op1=ALU.add),
                           r=[r_s[vi], r_c, rc], w=[rc])
                        for t in range(2):
                            op('dve', lambda e, cvb=cvb, cc=cc, st3=st3, t=t: e.scalar_tensor_tensor(
                                out=cvb[:, 0:16], in0=st3[:, :, t], scalar=vec["cw"][:, t, cc:cc + 1], in1=cvb[:, 0:16], op0=ALU.mult, op1=ALU.add),
                               r=[r_stT, r_c, rc], w=[rc])
                        op('pe', lambda e, ub=ub, q=q: e.transpose(pp[3][0:16, 0:128], ub[:, 2:18], cst["ident"][:]),
                           r=[r_s[vi], r_c], w=[r_ppb[3][0]])
                        op('act', lambda e, q=q: e.activation(out=outst[0:16, 128 * q:128 * (q + 1)], in_=pp[3][0:16, 0:128], func=AF.Copy),
                           r=[r_ppb[3][0]], w=[r_outst])
                    if g >= 1:
                        ffn_down(g - 1, part=2 * j + vi)
                op('act', lambda e: e.activation(out=scr[4][:, :nco], in_=scr[3][:, :nco], func=AF.Silu), r=[r_s[3]], w=[r_s[4]])
                op('pool', lambda e, j=j, actT=actT: e.tensor_tensor(out=actT[:, j, :nco], in0=scr[4][:, :nco], in1=scr[2][:, :nco], op=ALU.mult),
                   r=[r_s[2], r_s[4]], w=[r_actT])
            if s == 0:
                nr = 16
                if s == 0:
                    P.dma('act', O["ssconv"][:, 1, :].rearrange("b (v f) -> b v f", v=2)[:, :, 256 * g:256 * (g + 1)],
                          outst[0:16, :].rearrange("p (v f) -> p v f", v=2), r=[r_outst])
                else:
                    P.dma('act', O["pconv"].rearrange("b (v f) -> b v f", v=2)[:, :, 256 * g:256 * (g + 1)],
                          outst[0:2, :].rearrange("p (v f) -> p v f", v=2), r=[r_outst])

        def ffn_down(g, part=None):
            bd, rd = BDx[g % 2]
            actT, r_actT = actTx[g % 2], r_actTx[g % 2]
            units = [(slot, c0, n, nn) for (slot, c0, n) in tiles for nn in range(4)]
            if part is not None:
                per = (len(units) + 3) // 4
                units = units[per * part:per * (part + 1)]
            for (slot, c0, n, nn) in units:
                if True:
                    pidx = 2 + (nn % 2)
                    hb = nn // 2
                    for j in range(2):
                        op('pe', lambda e, j=j, c0=c0, n=n, nn=nn, pidx=pidx, hb=hb: e.matmul(
                            pp[pidx][:n, 512 * hb:512 * hb + 512], lhsT=actT[:, j, c0:c0 + n],
                            rhs=bd[:, j * 2048 + 512 * nn:j * 2048 + 512 * (nn + 1)], start=(j == 0), stop=(j == 1)),
                           r=[rd, r_actT], w=[r_ppb[pidx][hb]])
                    op('dve', lambda e, n=n, nn=nn, pidx=pidx, slot=slot, hb=hb: e.tensor_tensor(
                        out=xres[:n, slot, 512 * nn:512 * (nn + 1)], in0=pp[pidx][:n, 512 * hb:512 * hb + 512],
                        in1=xres[:n, slot, 512 * nn:512 * (nn + 1)], op=ALU.add), r=[r_ppb[pidx][hb], r_x[slot]], w=[r_x[slot]])

        if s == 0:
            bg_precast(66, flush=True)
        load_up(0)
        load_dn(0)
        load_up(1)
        load_dn(1)
        for g in range(22):
            ffn_up(g)
            if g + 2 < 22:
                load_up(g + 2)
            if g >= 1 and g + 1 < 22:
                load_dn(g + 1)
        ffn_down(21)
        if s == 3:
            for q8 in range(11):
                for i8 in range(8):
                    cc = 8 * q8 + i8
                    op('pe', lambda e, cc=cc, i8=i8: e.transpose(pp[2][0:2, 128 * i8:128 * (i8 + 1)], convst[:, cc, :], cst["ident"][:]),
                       r=[r_convst, r_c], w=[r_pp[2]])
                op('act', lambda e: e.activation(out=scr[0][0:2, 0:512], in_=pp[2][0:2, 0:512], func=AF.Copy), r=[r_pp[2]], w=[r_s[0]])
                op('act', lambda e: e.activation(out=scr[1][0:2, 0:512], in_=pp[2][0:2, 512:1024], func=AF.Copy), r=[r_pp[2]], w=[r_s[1]])
                P.dma('act', O["pconv"][:, 1024 * q8:1024 * q8 + 512], scr[0][0:2, 0:512], r=[r_s[0]])
                P.dma('act', O["pconv"][:, 1024 * q8 + 512:1024 * (q8 + 1)], scr[1][0:2, 0:512], r=[r_s[1]])
        for (slot, c0, n) in tiles:
            if s == 0 and slot == 0:
                P.dma('sp', O["y_s"][:, :], xres[0:16, 0, :], r=[r_x[0]])
            else:
                gi = y_rows[(s, slot)]
                P.dma('sp', O["y_p"][128 * gi:128 * (gi + 1), :], xres[:, slot, :], r=[r_x[slot]])
    P.dma('sp', O["ssconv"][:, 0, :], I["sconv"][:, 1, :])


_NC_CACHE = {}


def _consts():
    c = {}
    c["ident"] = np.eye(128, dtype=np.float32)
    rot = np.zeros((128, 128), np.float32)
    for m in range(128):
        if (m % 64) < 32:
            rot[m + 32, m] = -1.0
        else:
            rot[m - 32, m] = 1.0
    c["rotm"] = rot
    blk = np.zeros((128, 128), np.float32)
    blk[0:64, 0:64] = 1.0
    blk[64:128, 64:128] = 1.0
    c["blk64"] = blk
    c["ones"] = np.ones((128, 128), np.float32)
    k = np.arange(128)[:, None]
    q = np.arange(128)[None, :]
    c["maskc"] = (k <= q).astype(np.float32)
    c["maskp"] = (k > q).astype(np.float32)
    m48c = np.zeros((128, 128), np.float32)
    m48c[32:48, 32:48] = (k[0:16] <= q[:, 0:16]).astype(np.float32)
    c["m48c"] = m48c
    m48p = np.zeros((128, 128), np.float32)
    m48p[32:48, :] = (k[0:16] > (q - 112)).astype(np.float32)
    c["m48p"] = m48p
    posc = np.zeros((4, 560), np.float32)
    posc[0, 0:16] = 16384.0
    posc[0, 32:560] = np.arange(528)
    for s in range(1, 4):
        posc[s, 0:512] = 16 + 512 * s + np.arange(512)
    c["posc"] = posc
    inv = (10000.0 ** (-np.arange(32, dtype=np.float32) * 2.0 / 64)).astype(np.float32)
    c["ropeinv"] = np.tile(inv, 4).reshape(128, 1).astype(np.float32)
    return c


def prep_inputs(x_prompt, x_sample, cache_k, cache_v, state_ssm_re, state_ssm_im, state_conv,
                meta_tokens, norm_mix_g, w_in, q_norm_g, k_norm_g, attn_sinks, lam_re, lam_im, log_dt,
                ssm_b_re, ssm_b_im, ssm_c_re, ssm_c_im, ssm_d, w_glu, b_glu, ssm_out_g, attn_out_g,
                w_out, norm_ffn_g, w_up, conv_w, conv_b, w_down, cores=range(8)):
    f = lambda a: np.ascontiguousarray(np.asarray(a, dtype=np.float32))
    w_in0 = f(w_in)[0]
    qcols = np.concatenate([np.arange(1024 + 64 * h, 1024 + 64 * (h + 1)) for h in QHEAD_ORDER])
    w_in_p = np.concatenate([w_in0[:, 0:1024], w_in0[:, qcols], w_in0[:, 2048:2560]], axis=1)
    shared = dict(
        meta=f(meta_tokens), g_mix=f(norm_mix_g)[0].reshape(16, 128).T, g_ffn=f(norm_ffn_g)[0].reshape(16, 128).T,
        w_in=w_in_p, qg=np.tile(f(q_norm_g)[0], 2).reshape(128, 1), kg=np.tile(f(k_norm_g)[0], 2).reshape(128, 1),
        sinks=f(attn_sinks), lamre2=np.concatenate([f(lam_re)[0].T] * 2, 0), lamim2=np.concatenate([f(lam_im)[0].T] * 2, 0),
        logdt2=np.broadcast_to(f(log_dt)[0][None, :], (128, 64)),
        bre_t=f(ssm_b_re)[0].transpose(1, 0, 2), bim_t=f(ssm_b_im)[0].transpose(1, 0, 2),
        cre_t=f(ssm_c_re)[0].transpose(2, 0, 1), cim_t=f(ssm_c_im)[0].transpose(2, 0, 1),
        dsk=f(ssm_d)[0].reshape(8, 128).T, w_glu=f(w_glu)[0], b_glu=f(b_glu)[0].reshape(8, 128).T,
        g_s=f(ssm_out_g)[0].reshape(8, 128).T, g_a=f(attn_out_g)[0].reshape(8, 128).T,
        w_out=f(w_out)[0], w_up=f(w_up)[0], cw=f(conv_w)[0].reshape(3, 88, 128).transpose(2, 0, 1),
        cb=f(conv_b)[0].reshape(88, 128).T, w_down=f(w_down)[0],
    )
    shared.update(_consts())
    shared = {k: f(v) for k, v in shared.items()}
    in_maps = []
    for c in cores:
        m = dict(shared)
        sl = slice(16 * c, 16 * c + 16)
        m["xp"] = f(x_prompt[c])
        m["xs"] = f(x_sample[sl, 0])
        m["ck"] = f(cache_k[0, sl]).reshape(16, 128, 256)
        m["cv"] = f(cache_v[0, sl]).reshape(16, 128, 256)
        m["sre"] = f(state_ssm_re[0, sl])
        m["sim"] = f(state_ssm_im[0, sl])
        m["sconv"] = f(state_conv[0, sl])
        in_maps.append(m)
    return in_maps


def kernel(**inputs):
    if "nc" not in _NC_CACHE:
        _NC_CACHE["nc"] = build_nc()
    nc = _NC_CACHE["nc"]
    in_maps = prep_inputs(**inputs)
    res = run_bass_kernel_spmd(nc, in_maps, core_ids=list(range(8)))
    R = res.results
    cat = lambda n: np.concatenate([r[n] for r in R], axis=0)
    stk = lambda n: np.stack([r[n] for r in R], axis=0)
    y_p = stk("y_p")
    y_s = cat("y_s").reshape(128, 1, 2048)
    pck = stk("pck").reshape(1, 8, 128, 4, 64)
    pcv = stk("pcv").reshape(1, 8, 128, 4, 64)
    pre = stk("pre").reshape(1, 8, 64, 64)
    pim = stk("pim").reshape(1, 8, 64, 64)
    pconv = stk("pconv").reshape(1, 8, 2, 11264)
    sck = cat("sck").reshape(1, 128, 128, 4, 64)
    scv = cat("scv").reshape(1, 128, 128, 4, 64)
    ssre = cat("ssre").reshape(1, 128, 64, 64)
    ssim = cat("ssim").reshape(1, 128, 64, 64)
    ssconv = cat("ssconv").reshape(1, 128, 2, 11264)
    return tuple(np.ascontiguousarray(a, dtype=np.float32) for a in
                 (y_p, y_s, pck, pcv, pre, pim, pconv, sck, scv, ssre, ssim, ssconv))
```

```python
import contextlib
import math
import numpy as np
import ml_dtypes
import concourse.bass as bass
import concourse.mybir as mybir
from concourse.bass_utils import run_bass_kernel_spmd

F32 = mybir.dt.float32
BF16 = mybir.dt.bfloat16
I32 = mybir.dt.int32
ALU = mybir.AluOpType
AF = mybir.ActivationFunctionType

ENGS = ['pe', 'act', 'dve', 'pool', 'sp']
NDMASEM = 8
EPS = 1e-6
TWO_PI = 2.0 * math.pi
MAGIC = 12582912.0


class Res:
    __slots__ = ('name', 'w', 'rs', 'kids')

    def __init__(self, name='', kids=None):
        self.name = name
        self.w = None
        self.rs = []
        self.kids = kids


def _flat(lst):
    out = []
    for x in lst:
        if x.kids:
            out.extend(x.kids)
        else:
            out.append(x)
    return out


class Op:
    __slots__ = ('eng', 'fn', 'deps', 'signal', 'sem', 'val', 'ndma', 'idx', 'calls')


class Rec:
    def __init__(self):
        self.calls = []

    def __getattr__(self, name):
        def f(*a, **k):
            self.calls.append((name, a, k))
            return None
        return f


class Prog:
    def __init__(self, nc, es):
        self.nc = nc
        self.es = es
        self.ops = {e: [] for e in ENGS}
        self.n = 0
        self.csem = {e: es.enter_context(nc.semaphore('c_' + e)) for e in ENGS}
        self.dsem = {e: [es.enter_context(nc.semaphore('d_%s%d' % (e, i))) for i in range(NDMASEM)]
                     for e in ('sp', 'pool', 'act')}
        self.dtot = {e: [0] * NDMASEM for e in ('sp', 'pool', 'act')}
        self.dlast = {e: [None] * NDMASEM for e in ('sp', 'pool', 'act')}
        self.drr = {e: 0 for e in ('sp', 'pool', 'act')}

    def sb(self, name, shape, dt):
        return self.es.enter_context(self.nc.sbuf_tensor(name, list(shape), dt))

    def ps(self, name, shape, dt=F32):
        return self.es.enter_context(self.nc.psum_tensor(name, list(shape), dt))

    def op(self, eng, fn, r=(), w=(), ndma=0):
        o = Op()
        o.eng = eng
        o.fn = fn
        rec = Rec()
        fn(rec)
        o.calls = rec.calls
        o.ndma = ndma
        o.signal = False
        o.sem = None
        o.val = 0
        o.idx = self.n
        self.n += 1
        deps = {}
        r = _flat(r)
        w = _flat(w)
        w = list(w) + [x for x in r if x.name.startswith('pp') and x not in w]
        r = [x for x in r if not x.name.startswith('pp')]
        for x in r:
            if x.w is not None:
                deps[x.w.idx] = x.w
        for x in w:
            if x.w is not None:
                deps[x.w.idx] = x.w
            for q in x.rs:
                deps[q.idx] = q
        if ndma:
            k = self.drr[eng]
            self.drr[eng] = (k + 1) % NDMASEM
            prev = self.dlast[eng][k]
            if prev is not None:
                deps[prev.idx] = prev
            self.dtot[eng][k] += 16 * ndma
            o.sem = self.dsem[eng][k]
            o.val = self.dtot[eng][k]
            o.signal = True
            self.dlast[eng][k] = o
        dl = []
        for d in deps.values():
            if d.eng == 'pe' and eng == 'pe' and not d.ndma and not ndma:
                continue
            dl.append(d)
        o.deps = dl
        for x in r:
            x.rs.append(o)
        for x in w:
            x.w = o
            x.rs = []
        self.ops[eng].append(o)
        return o

    def dma(self, eng, out, in_, r=(), w=(), **kw):
        return self.op(eng, lambda e: [e.dma_start(out=out, in_=in_, **kw)], r=r, w=w, ndma=1)

    def emit(self):
        nc = self.nc
        for e in ENGS:
            for o in self.ops[e]:
                for d in o.deps:
                    d.signal = True
        finals = []
        for e in ENGS:
            c = 0
            for o in self.ops[e]:
                if o.ndma:
                    continue
                if o.signal:
                    c += 1
                    o.val = c
                    o.sem = self.csem[e]
            if c and e != 'sp':
                finals.append((self.csem[e], c))
        for e in ('sp', 'pool', 'act'):
            for k in range(NDMASEM):
                if self.dtot[e][k]:
                    finals.append((self.dsem[e][k], self.dtot[e][k]))

        def mk(e):
            def body(engobj):
                known = {}
                for o in self.ops[e]:
                    for d in o.deps:
                        key = id(d.sem)
                        if known.get(key, 0) >= d.val:
                            continue
                        engobj.wait_ge(d.sem, d.val)
                        known[key] = d.val
                    ins = [getattr(engobj, nm)(*a, **k) for nm, a, k in o.calls]
                    if o.ndma:
                        assert len(ins) == o.ndma
                        for i in ins:
                            i.then_inc(o.sem, 16)
                    elif o.signal:
                        ins[-1].then_inc(o.sem, 1)
                if e == 'sp':
                    for s, v in finals:
                        engobj.wait_ge(s, v)
            return body

        with nc.Block() as block:
            block.tensor(mk('pe'))
            block.scalar(mk('act'))
            block.vector(mk('dve'))
            block.gpsimd(mk('pool'))
            block.sync(mk('sp'))


D = 2048
DFF = 5632
NFC = 44
NCOL = [560, 512, 512, 512]
QHEAD_ORDER = [0, 4, 1, 5, 2, 6, 3, 7, 8, 12, 9, 13, 10, 14, 11, 15]

IN_SPECS = [
    ("xp", [2048, 2048]), ("xs", [16, 2048]), ("meta", [16, 2048]),
    ("ck", [16, 128, 256]), ("cv", [16, 128, 256]), ("sre", [16, 64, 64]), ("sim", [16, 64, 64]),
    ("sconv", [16, 2, 11264]),
    ("g_mix", [128, 16]), ("g_ffn", [128, 16]), ("w_in", [2048, 2560]), ("qg", [128, 1]), ("kg", [128, 1]),
    ("sinks", [1, 16]), ("lamre2", [128, 64]), ("lamim2", [128, 64]), ("logdt2", [128, 64]),
    ("bre_t", [64, 64, 16]), ("bim_t", [64, 64, 16]), ("cre_t", [64, 64, 16]), ("cim_t", [64, 64, 16]),
    ("dsk", [128, 8]), ("w_glu", [1024, 1024]), ("b_glu", [128, 8]), ("g_s", [128, 8]), ("g_a", [128, 8]),
    ("w_out", [2048, 2048]), ("w_up", [2048, 11264]), ("cw", [128, 3, 88]), ("cb", [128, 88]),
    ("w_down", [5632, 2048]),
    ("ident", [128, 128]), ("rotm", [128, 128]), ("blk64", [128, 128]), ("ones", [128, 128]),
    ("maskc", [128, 128]), ("maskp", [128, 128]), ("m48c", [128, 128]), ("m48p", [128, 128]),
    ("posc", [4, 560]), ("ropeinv", [128, 1]),
]
OUT_SPECS = [
    ("y_p", [2048, 2048]), ("y_s", [16, 2048]), ("pck", [128, 256]), ("pcv", [128, 256]),
    ("pre", [64, 64]), ("pim", [64, 64]), ("pconv", [2, 11264]),
    ("sck", [16, 128, 256]), ("scv", [16, 128, 256]), ("ssre", [16, 64, 64]), ("ssim", [16, 64, 64]),
    ("ssconv", [16, 2, 11264]),
]


def st_tiles(s):
    if s == 0:
        return [(0, 0, 48)] + [(i + 1, 48 + 128 * i, 128) for i in range(4)]
    return [(i, 128 * i, 128) for i in range(4)]


def st_chunks(s):
    return [(0, 48), (48, 512)] if s == 0 else [(0, 512)]


CFG = dict(nst=4, stage=99, taps=())
TAPS = {}


def build_nc():
    nc = bass.Bass("TRN2", target_bir_lowering=False)
    TAPS.clear()
    I = {n: nc.dram_tensor(n, sh, F32, kind="ExternalInput").ap() for n, sh in IN_SPECS}
    O = {n: nc.dram_tensor(n, sh, F32, kind="ExternalOutput").ap() for n, sh in OUT_SPECS}
    with contextlib.ExitStack() as es:
        P = Prog(nc, es)
        build_body(nc, P, I, O)
        P.emit()
    return nc


def build_body(nc, P, I, O):
    op = P.op
    R = {}

    def tap(name, ap, rr):
        if name.rstrip("0123456789") not in CFG["taps"]:
            return
        shape = list(ap.shape)
        d = nc.dram_tensor("tap_" + name, shape, F32, kind="ExternalOutput").ap()
        TAPS[name] = shape
        idx = tuple(slice(None) for _ in shape)
        P.dma('pool', d[idx], ap, r=rr)

    def res(n):
        if n not in R:
            R[n] = Res(n)
        return R[n]

    xres = P.sb("xres", [128, 5, 2048], F32)
    r_x = [res("x%d" % i) for i in range(5)]
    hT = P.sb("hT", [128, 16, 560], BF16)
    r_hT = res("hT")
    NW = 5
    wb = [P.sb("wb%d" % i, [128, 4096], BF16) for i in range(NW)]
    r_wb = [res("wb%d" % i) for i in range(NW)]
    wrr = [0]
    uT = P.sb("uT", [128, 8, 560], BF16)
    r_u = [res("u%d" % i) for i in range(8)]
    QT = P.sb("QT", [128, 8, 560], BF16)
    r_Q = res("QT")
    KT = P.sb("KT", [128, 2, 688], BF16)
    r_K = res("KT")
    V1 = P.sb("V1", [128, 6, 4, 65], BF16)
    r_V = [res("V%d" % i) for i in range(6)]
    NS = 8
    scr = [P.sb("scr%d" % i, [128, 562], F32) for i in range(NS)]
    r_s = [res("scr%d" % i) for i in range(NS)]
    W1b = P.sb("W1b", [128, 560], BF16)
    W2b = P.sb("W2b", [128, 560], BF16)
    r_W1, r_W2 = res("W1b"), res("W2b")
    W1x = [W1b, P.sb("W1b2", [128, 560], BF16)]
    W2x = [W2b, P.sb("W2b2", [128, 560], BF16)]
    r_W1x = [r_W1, res("W1b2")]
    r_W2x = [r_W2, res("W2b2")]
    sqb = P.sb("sqb", [128, 2048], BF16)
    r_sqb = res("sqb")
    actTx = [P.sb("actT%d" % i, [128, 2, 560], BF16) for i in range(2)]
    r_actTx = [res("actT%d" % i) for i in range(2)]
    rcos_t = P.sb("rcos", [128, 562], F32)
    rsin_t = P.sb("rsin", [128, 562], F32)
    rcos, rsin = rcos_t, rsin_t
    scr += [rcos_t, rsin_t]
    r_s += [res("rcos"), res("rsin")]
    TI = P.sb("TI", [128, 560], F32)
    r_rope = Res("rope", kids=[r_s[8], r_s[9]])
    r_TI, r_rstdb = res("TI"), res("rstdb")
    Bst = [P.sb("Bst%d" % i, [128, 16, 128], BF16) for i in range(2)]
    Cst = [P.sb("Cst%d" % i, [128, 64, 32], BF16) for i in range(2)]
    Dd = P.sb("Dd", [128, 8, 128], BF16)
    r_ssmc = res("ssmc")
    r_zl = res("zl")
    r_sck, r_scv = res("sck"), res("scv")
    tab = {k: P.sb("tab_" + k, [128, 64], F32) for k in
           ["lr", "li", "dt", "th", "mag", "ar", "ai", "cre", "cim", "cres", "cims", "t1", "t2", "t3", "zl"]}
    att = P.sb("att", [128, 16, 64], F32)
    attn = P.sb("attn", [128, 1024], BF16)
    r_att, r_attn = res("att"), res("attn")
    PTc = P.sb("PTc", [128, 4, 128], BF16)
    PTp = P.sb("PTp", [128, 4, 128], BF16)
    r_PTc, r_PTp = res("PTc"), res("PTp")
    fin = P.sb("fin", [128, 128], F32)
    r_fin = res("fin")
    small = P.sb("small", [128, 64], F32)
    r_small = res("small")
    esink = P.sb("esink", [128, 16], F32)
    cst = {k: P.sb("c_" + k, [128, 128], F32) for k in ["ident"]}
    cbf = {k: P.sb("b_" + k, [128, 128], BF16) for k in
           ["ident", "rotm", "blk64", "ones", "maskc", "maskp", "m48c", "m48p"]}
    vec = {k: P.sb("v_" + k, sh, F32) for k, sh in
           [("g_mix", [128, 16]), ("g_ffn", [128, 16]), ("qg", [128, 1]), ("kg", [128, 1]), ("dsk", [128, 8]),
            ("b_glu", [128, 8]), ("g_s", [128, 8]), ("g_a", [128, 8]), ("cw", [128, 3, 88]), ("cb", [128, 88]),
            ("ropeinv", [128, 1])]}
    r_c = res("consts")
    convst = P.sb("convst", [128, 88, 2], F32)
    r_convst = res("convst")
    supb = P.sb("supb", [128, 4, 16], F32)
    r_supb = res("supb")
    stT = P.sb("stT", [128, 4, 32], F32)
    r_stT = res("stT")
    h0tok = att[0:16].rearrange("p a b -> p (a b)").rearrange("p (a b) -> p a b", a=8)
    h0tok2 = sqb[0:16, :].bitcast(F32).rearrange("p (a b) -> p a b", a=8)
    h0T = P.sb("h0T", [128, 8, 16], F32)
    h0T2 = P.sb("h0T2", [128, 8, 16], F32)
    sT = P.sb("sT", [128, 8, 16], F32)
    r_h0 = r_att
    R["sqb"] = r_att
    r_sqb = r_att
    KVb = P.sb("KVb", [128, 4160], BF16)
    Kb = KVb[:, 0:2048].rearrange("p (a b) -> p a b", a=8)
    Vb = KVb[:, 2048:4128].rearrange("p (a b c) -> p a b c", a=8, b=4)
    KbT = W2b[:, 0:256].rearrange("p (a b) -> p a b", a=2)
    r_Kb = res("Kb")
    r_Vb, r_KbT = r_Kb, r_W2
    tokst = att[0:32].rearrange("p a b -> p (a b)")[:, 0:512]
    r_tokst = r_att
    wb.append(KVb[:, 0:4096])
    r_wb.append(r_Kb)
    PTs = W1b[:, 0:256]
    r_PTs = r_W1
    OTs = scr[7][0:65, 0:256]
    r_OTs = r_s[7]
    outst = P.sb("outst", [128, 512], F32)
    r_outst = res("outst")
    pp = [P.ps("pp%d" % i, [128, 1024], F32) for i in range(4)]
    r_ppb = [[res("pp%d_%d" % (i, b)) for b in range(2)] for i in range(4)]
    r_pp = [Res("ppc%d" % i, kids=r_ppb[i]) for i in range(4)]

    def pbf(i):
        return pp[i][:].bitcast(BF16)

    for k in cst:
        P.dma('sp', cst[k][:], I[k][:, :], w=[r_c])
    for k in cbf:
        P.dma('pool', cbf[k][:], I[k][:, :], w=[r_c])
    for k in vec:
        src = I[k]
        P.dma('sp', vec[k][:], src[:] if len(src.shape) == 2 else src[:, :, :], w=[r_c])
    P.dma('sp', esink[:], I["sinks"].partition_broadcast(128), w=[r_c])
    op('act', lambda e: e.activation(out=esink[:], in_=esink[:], func=AF.Exp), r=[r_c], w=[r_c])
    op('dve', lambda e: e.memset(small[:], 0.0), w=[r_small])
    op('dve', lambda e: e.memset(small[:, 60:61], EPS), r=[r_small], w=[r_small])
    op('dve', lambda e: e.memset(small[:, 61:62], math.pi / 2), r=[r_small], w=[r_small])
    op('dve', lambda e: e.memset(small[:, 58:59], MAGIC), r=[r_small], w=[r_small])
    op('dve', lambda e: e.memset(small[:, 59:60], -MAGIC), r=[r_small], w=[r_small])
    op('dve', lambda e: e.tensor_scalar(out=small[:, 62:63], in0=vec["ropeinv"][:], scalar1=1.0 / TWO_PI, scalar2=None,
                                          op0=ALU.mult), r=[r_small, r_c], w=[r_small])
    EPS_AP = small[:, 60:61]
    HPI_AP = small[:, 61:62]
    RINV_AP = small[:, 62:63]
    M_AP = small[:, 58:59]
    NM_AP = small[:, 59:60]
    op('dve', lambda e: e.memset(convst[:], 0.0), w=[r_convst])
    op('pool', lambda e: e.memset(V1[:], 1.0), w=r_V)
    op('pool', lambda e: e.memset(Vb, 1.0), w=[r_Vb])
    op('pool', lambda e: e.memset(KT[:], 0.0), w=[r_K])

    def sincos(src_ap, scale_ap, n, cos_out, sin_out, rr, ww, tmp):
        a, b = tmp
        op('act', lambda e: e.activation(out=scr[a][:, :n], in_=src_ap, func=AF.Identity, scale=scale_ap, bias=M_AP),
           r=rr + [r_small, r_ssmc], w=[r_s[a]])
        op('act', lambda e: e.activation(out=scr[a][:, :n], in_=scr[a][:, :n], func=AF.Identity, bias=NM_AP),
           r=[r_s[a], r_small], w=[r_s[a]])
        op('dve', lambda e: e.scalar_tensor_tensor(out=scr[b][:, :n], in0=src_ap, scalar=scale_ap, in1=scr[a][:, :n],
                                                    op0=ALU.mult, op1=ALU.subtract), r=rr + [r_s[a], r_small, r_ssmc], w=[r_s[b]])
        op('act', lambda e: e.activation(out=sin_out, in_=scr[b][:, :n], func=AF.Sin, scale=TWO_PI),
           r=[r_s[b]], w=ww)
        op('act', lambda e: e.activation(out=scr[a][:, :n], in_=scr[b][:, :n], func=AF.Abs), r=[r_s[b]], w=[r_s[a]])
        op('act', lambda e: e.activation(out=cos_out, in_=scr[a][:, :n], func=AF.Sin, scale=-TWO_PI, bias=HPI_AP),
           r=[r_s[a], r_small], w=ww)

    def wload(srcs):
        k = wrr[0]
        wrr[0] = (k + 1) % NW
        buf = wb[k]
        for f, s in srcs:
            P.dma('pool', f(buf), s, w=[r_wb[k]])
        return buf, r_wb[k]

    wv_ = I["w_in"].rearrange("(k p) n -> p k n", p=128)
    wg_ = I["w_glu"].rearrange("(k p) n -> p k n", p=128)
    wo_ = I["w_out"].rearrange("(k p) n -> p k n", p=128)
    wu_ = I["w_up"].rearrange("(k p) n -> p k n", p=128)
    wd_ = I["w_down"].rearrange("(k p) n -> p k n", p=128)
    sc_in = nc.dram_tensor("sc_in", [10, 128, 4096], BF16).ap()
    sc_glu = nc.dram_tensor("sc_glu", [4, 128, 2048], BF16).ap()
    sc_out = nc.dram_tensor("sc_out", [8, 128, 4096], BF16).ap()
    sc_ffn = nc.dram_tensor("sc_ffn", [66, 128, 4096], BF16).ap()
    r_sc = {}
    pc_list = []
    for pi in range(10):
        pc_list.append((sc_in[pi].rearrange("p (k n) -> p k n", k=16), wv_[:, :, 256 * pi:256 * (pi + 1)], ('in', pi)))
    for i4 in range(4):
        pc_list.append((sc_glu[i4].rearrange("p (k n) -> p k n", k=8), wg_[:, :, 256 * i4:256 * (i4 + 1)], ('glu', i4)))
    for pi in range(8):
        pc_list.append((sc_out[pi].rearrange("p (k n) -> p k n", k=16), wo_[:, :, 256 * pi:256 * (pi + 1)], ('out', pi)))
    for g in range(22):
        pc_list.append((sc_ffn[3 * g].rearrange("p (k n) -> p k n", k=16), wu_[:, :, 256 * g:256 * (g + 1)], ('ffn', 3 * g)))
        pc_list.append((sc_ffn[3 * g + 1].rearrange("p (k n) -> p k n", k=16), wu_[:, :, DFF + 256 * g:DFF + 256 * (g + 1)], ('ffn', 3 * g + 1)))
        pc_list.append((sc_ffn[3 * g + 2].rearrange("p (k n) -> p k n", k=2), wd_[:, 2 * g:2 * g + 2, :], ('ffn', 3 * g + 2)))
    for (_, _, key) in pc_list:
        r_sc[key] = res("sc_%s%d" % key)

    def emit_precast(n):
        for _ in range(n):
            if not pc_list:
                return
            dst, src, key = pc_list.pop(0)
            P.dma('pool', dst, src, w=[r_sc[key]])

    qtog = [0]

    def wfill(s, buf, rbuf, sc2d, key, width, src3d, kk):
        if s == 0:
            P.dma('pool', buf[:, 0:width].rearrange("p (k n) -> p k n", k=kk), src3d, w=[rbuf])
            P.dma('sp', sc2d, buf[:, 0:width], r=[rbuf], w=[r_sc[key]])
        else:
            q = ('sp', 'pool')[qtog[0]]
            qtog[0] = 1 - qtog[0]
            P.dma(q, buf[:, 0:width], sc2d, r=[r_sc[key]], w=[rbuf])

    bg_list = []
    for g in range(22):
        bg_list.append((sc_ffn[3 * g], ('ffn', 3 * g), wu_[:, :, 256 * g:256 * (g + 1)], 16))
        bg_list.append((sc_ffn[3 * g + 1], ('ffn', 3 * g + 1), wu_[:, :, DFF + 256 * g:DFF + 256 * (g + 1)], 16))
        bg_list.append((sc_ffn[3 * g + 2], ('ffn', 3 * g + 2), wd_[:, 2 * g:2 * g + 2, :], 2))
    bg_rr = [0]

    def bg_precast(n, flush=False):
        for _ in range(n):
            if not bg_list:
                return
            sc2d, key, src3d, kk = bg_list.pop(0)
            b = 3 + bg_rr[0]
            bg_rr[0] = 1 - bg_rr[0]
            P.dma('pool', wb[b][:].rearrange("p (k n) -> p k n", k=kk), src3d, w=[r_wb[b]])
            P.dma('sp', sc2d, wb[b][:], r=[r_wb[b]], w=[r_sc[key]])

    def wload_sc(s, sc2d, key, width, src3d, kk):
        k = wrr[0]
        wrr[0] = (k + 1) % 3
        wfill(s, wb[k], r_wb[k], sc2d, key, width, src3d, kk)
        return wb[k], r_wb[k]

    mix_specs = []
    for pi in range(10):
        mix_specs.append((sc_in[pi], ('in', pi), 4096, wv_[:, :, 256 * pi:256 * (pi + 1)], 16))
    for i4 in range(4):
        mix_specs.append((sc_glu[i4], ('glu', i4), 2048, wg_[:, :, 256 * i4:256 * (i4 + 1)], 8))
    for pi in range(8):
        mix_specs.append((sc_out[pi], ('out', pi), 4096, wo_[:, :, 256 * pi:256 * (pi + 1)], 16))
    mix_state = {}

    def mget(s, i):
        st = mix_state.setdefault(s, {'next': 0, 'got': {}})
        while st['next'] < min(len(mix_specs), i + 3):
            st['got'][st['next']] = wload_sc(s, *mix_specs[st['next']])
            st['next'] += 1
        return st['got'][i]

    T = tab
    P.dma('sp', T["lr"][:], I["lamre2"][:, :], w=[r_ssmc])
    P.dma('sp', T["li"][:], I["lamim2"][:, :], w=[r_ssmc])
    P.dma('sp', T["dt"][:], I["logdt2"][:, :], w=[r_ssmc])
    S1 = [r_ssmc]

    def tt(o, a, b, f, eng='dve'):
        op(eng, lambda e: e.tensor_tensor(out=o, in0=a, in1=b, op=f), r=S1, w=S1)

    op('act', lambda e: e.activation(out=T["dt"][:], in_=T["dt"][:], func=AF.Exp), r=S1, w=S1)
    tt(T["t1"][:], T["lr"][:], T["dt"][:], ALU.mult)
    op('act', lambda e: e.activation(out=T["mag"][:], in_=T["t1"][:], func=AF.Exp), r=S1, w=S1)
    tt(T["th"][:], T["li"][:], T["dt"][:], ALU.mult)
    op('dve', lambda e: e.tensor_scalar(out=T["th"][:], in0=T["th"][:], scalar1=1.0 / TWO_PI, scalar2=None, op0=ALU.mult),
       r=S1, w=S1)
    op('dve', lambda e: e.memset(scr[2][:, 0:1], 1.0), w=[r_s[2]])
    sincos(T["th"][:], scr[2][:, 0:1], 64, T["ar"][:], T["ai"][:], S1 + [r_s[2]], S1, (0, 1))
    tt(T["ar"][:], T["ar"][:], T["mag"][:], ALU.mult)
    tt(T["ai"][:], T["ai"][:], T["mag"][:], ALU.mult)
    tt(T["t1"][:], T["lr"][:], T["lr"][:], ALU.mult)
    tt(T["t2"][:], T["li"][:], T["li"][:], ALU.mult)
    tt(T["t1"][:], T["t1"][:], T["t2"][:], ALU.add)
    op('dve', lambda e: e.reciprocal(out=T["t1"][:], in_=T["t1"][:]), r=S1, w=S1)
    op('dve', lambda e: e.tensor_scalar(out=T["t2"][:], in0=T["ar"][:], scalar1=-1.0, scalar2=None, op0=ALU.add),
       r=S1, w=S1)
    tt(T["cre"][:], T["t2"][:], T["lr"][:], ALU.mult)
    tt(T["t3"][:], T["ai"][:], T["li"][:], ALU.mult)
    tt(T["cre"][:], T["cre"][:], T["t3"][:], ALU.add)
    tt(T["cre"][:], T["cre"][:], T["t1"][:], ALU.mult)
    tt(T["cim"][:], T["ai"][:], T["lr"][:], ALU.mult)
    tt(T["t3"][:], T["t2"][:], T["li"][:], ALU.mult)
    tt(T["cim"][:], T["cim"][:], T["t3"][:], ALU.subtract)
    tt(T["cim"][:], T["cim"][:], T["t1"][:], ALU.mult)
    op('dve', lambda e: e.tensor_copy(out=T["cres"][0:64, :], in_=T["cre"][0:64, :]), r=S1, w=S1)
    op('dve', lambda e: e.tensor_scalar(out=T["cres"][64:128, :], in0=T["cre"][64:128, :], scalar1=-1.0, scalar2=None,
                                          op0=ALU.mult), r=S1, w=S1)
    op('dve', lambda e: e.tensor_copy(out=T["cims"][64:128, :], in_=T["cim"][64:128, :]), r=S1, w=S1)
    op('dve', lambda e: e.tensor_scalar(out=T["cims"][0:64, :], in0=T["cim"][0:64, :], scalar1=-1.0, scalar2=None,
                                          op0=ALU.mult), r=S1, w=S1)
    op('dve', lambda e: e.memset(T["zl"][:], 0.0), r=S1, w=S1)
    S1 = [r_ssmc, r_wb[0], r_wb[1], r_wb[2]]
    TB1 = wb[0][:, 0:2048].bitcast(F32).rearrange("p (a b) -> p a b", b=16)
    TB2 = wb[0][:, 2048:4096].bitcast(F32).rearrange("p (a b) -> p a b", b=16)
    BS = wb[1][:, 0:2048].bitcast(F32).rearrange("p (a b) -> p a b", b=16)
    Bin = wb[2][:, 0:4096].bitcast(F32).rearrange("p (a b) -> p a b", b=128)
    P.dma('sp', TB1[0:64], I["bre_t"][:, :, :], w=S1)
    P.dma('sp', TB1[64:128], I["bim_t"][:, :, :], w=S1)
    P.dma('sp', TB2[0:64], I["bim_t"][:, :, :], w=S1)
    P.dma('sp', TB2[64:128], I["bre_t"][:, :, :], w=S1)

    def bc(t):
        return t[:].unsqueeze(2).to_broadcast([128, 64, 16])

    for v in range(2):
        ca, cb_ = (T["cre"], T["cims"]) if v == 0 else (T["cim"], T["cres"])
        tt(BS, TB1, bc(ca), ALU.mult)
        tt(TB1 if False else Bin.rearrange("p a (b c) -> p (a b) c", c=16)[:, 0:64, :], TB2, bc(cb_), ALU.mult)
        tt(BS, BS, Bin.rearrange("p a (b c) -> p (a b) c", c=16)[:, 0:64, :], ALU.add)
        op('dve', lambda e: e.memset(Bin, 0.0), r=S1, w=S1)
        for s_ in range(2):
            src = BS.rearrange("p (c j s) h -> p c j s h", c=8, j=4, s=2)[:, :, :, s_, :]
            dst = Bin.rearrange("p (c s) (j t h) -> p c s j t h", s=2, j=4, t=2)[:, :, s_, :, s_, :]
            op('dve', lambda e, src=src, dst=dst: e.tensor_copy(out=dst, in_=src), r=S1, w=S1)
        for q4 in range(4):
            for j in range(4):
                cs = q4 * 4 + j
                op('pe', lambda e, cs=cs, j=j: e.transpose(pp[0][:, 128 * j:128 * (j + 1)], Bin[:, cs, :], cst["ident"][:]),
                   r=S1 + [r_c], w=[r_pp[0]])
            op('act', lambda e, q4=q4, v=v: e.activation(out=Bst[v][:, 4 * q4:4 * q4 + 4, :],
                                                           in_=pp[0][:, 0:512].rearrange("p (a b) -> p a b", a=4), func=AF.Copy),
               r=[r_pp[0]], w=S1)
    P.dma('sp', TB1[0:64], I["cre_t"][:, :, :], r=S1, w=S1)
    P.dma('sp', TB1[64:128], I["cim_t"][:, :, :], r=S1, w=S1)
    P.dma('sp', TB2[0:64], I["cim_t"][:, :, :], r=S1, w=S1)
    P.dma('sp', TB2[64:128], I["cre_t"][:, :, :], r=S1, w=S1)
    op('dve', lambda e: e.tensor_scalar(out=TB1[64:128], in0=TB1[64:128], scalar1=-1.0, scalar2=None, op0=ALU.mult), r=S1, w=S1)
    op('dve', lambda e: e.tensor_scalar(out=TB2, in0=TB2, scalar1=-1.0, scalar2=None, op0=ALU.mult), r=S1, w=S1)
    for v, TBx in enumerate((TB1, TB2)):
        op('pool', lambda e, v=v: e.memset(Cst[v][:], 0.0), r=S1, w=S1)
        for s_ in range(2):
            src = TBx.rearrange("p (a s) h -> p a s h", s=2)[:, :, s_, :]
            dst = Cst[v][:].rearrange("p (a s) (t h) -> p a s t h", s=2, t=2)[:, :, s_, s_, :]
            op('dve', lambda e, src=src, dst=dst: e.tensor_copy(out=dst, in_=src), r=S1, w=S1)
    for c in range(8):
        op('dve', lambda e, c=c: e.tensor_scalar(out=Dd[:, c, :], in0=cbf["ident"][:], scalar1=vec["dsk"][:, c:c + 1],
                                                   scalar2=None, op0=ALU.mult), r=S1 + [r_c], w=S1)

    def norm_to_hT(s, gname):
        for (slot, c0, n) in st_tiles(s):
            op('dve', lambda e, n=n: e.memset(small[:n, 4:6], 0.0), r=[r_small], w=[r_small])
            for hh in range(2):
                op('act', lambda e, slot=slot, n=n, hh=hh: e.activation(out=pp[3][:n, 0:1024], in_=xres[:n, slot, 1024 * hh:1024 * (hh + 1)],
                                                                     func=AF.Square, accum_out=small[:n, 4 + hh:5 + hh]),
                   r=[r_x[slot], r_small], w=[r_pp[3], r_small])
            op('dve', lambda e, n=n: e.tensor_tensor(out=small[:n, 0:1], in0=small[:n, 4:5], in1=small[:n, 5:6], op=ALU.add),
               r=[r_small], w=[r_small])
            op('act', lambda e, n=n: e.activation(out=small[:n, 1:2], in_=small[:n, 0:1], func=AF.Sqrt, scale=1.0 / D,
                                                   bias=EPS_AP[:n]), r=[r_small], w=[r_small])
            op('dve', lambda e, n=n: e.reciprocal(out=small[:n, 1:2], in_=small[:n, 1:2]), r=[r_small], w=[r_small])
            op('act', lambda e, slot=slot, n=n: e.activation(out=sqb[:n, :], in_=xres[:n, slot, :], func=AF.Copy, scale=small[:n, 1:2]),
               r=[r_x[slot], r_small], w=[r_sqb])
            for q4 in range(4):
                pv = pbf(q4 % 2)
                for j in range(4):
                    k = q4 * 4 + j
                    op('pe', lambda e, k=k, j=j, pv=pv, n=n: e.transpose(
                        pv[:, 128 * j:128 * j + n], sqb[:n, 128 * k:128 * (k + 1)], cbf["ident"][:n, :n]),
                       r=[r_sqb, r_c], w=[r_ppb[q4 % 2][0]])
                op('dve', lambda e, q4=q4, pv=pv, n=n, c0=c0: e.tensor_tensor(
                    out=hT[:, 4 * q4:4 * q4 + 4, c0:c0 + n], in0=pv[:, 0:512].rearrange("p (a b) -> p a b", a=4)[:, :, :n],
                    in1=vec[gname][:, 4 * q4:4 * q4 + 4].unsqueeze(2).to_broadcast([128, 4, n]), op=ALU.mult),
                   r=[r_ppb[q4 % 2][0], r_c], w=[r_hT])

    def proj_fm(s, wbuf, wr, j, width, pidx):
        nco = NCOL[s]
        po = 1024 - nco
        for (c0, n) in st_chunks(s):
            for k in range(16):
                op('pe', lambda e, k=k, c0=c0, n=n: e.matmul(
                    pp[pidx][:, po + c0:po + c0 + n], lhsT=wbuf[:, k * width + 128 * j:k * width + 128 * (j + 1)],
                    rhs=hT[:, k, c0:c0 + n], start=(k == 0), stop=(k == 15)),
                   r=[wr, r_hT], w=[r_pp[pidx]])
        return pp[pidx][:, po:1024]

    y_rows = {}

    for s in range(CFG['nst']):
        nco = NCOL[s]
        po = 1024 - nco
        tiles = st_tiles(s)
        PEL = 'dve' if s == 0 else 'pool'
        pc0 = 32 if s == 0 else 0
        for (slot, c0, n) in tiles:
            if s == 0 and slot == 0:
                op(PEL, lambda e: e.memset(xres[:, 0, :], 0.0), w=[r_x[0]])
                P.dma('sp', xres[0:16, 0, :], I["xs"][:, :], w=[r_x[0]])
                P.dma('sp', xres[32:48, 0, :], I["meta"][:, :], w=[r_x[0]])
            else:
                gi = 4 * s + (slot - 1 if s == 0 else slot)
                y_rows[(s, slot)] = gi
                P.dma('sp', xres[:, slot, :], I["xp"][128 * gi:128 * (gi + 1), :], w=[r_x[slot]])
        P.dma('sp', TI[:, :nco], I["posc"][s:s + 1, 0:nco].partition_broadcast(128), w=[r_TI])
        sincos(TI[:, :nco], RINV_AP, nco, rcos[:, :nco], rsin[:, :nco], [r_TI], [r_rope], (0, 1))
        norm_to_hT(s, "g_mix")
        tap('hT%d' % s, hT[:, :, :nco], [r_hT])
        tap('rcos%d' % s, rcos[:, :nco], [r_rope])
        if CFG['stage'] < 2:
            continue
        tasks = [(pi, j) for pi in range(9) for j in range(2)]

        def do_proj(t):
            pi, j = tasks[t]
            wbuf, wr = mget(s, pi)
            if s == 0 and j == 0:
                bg_precast(2)
            return proj_fm(s, wbuf, wr, j, 256, j)

        pz_next = do_proj(0)
        for t in range(len(tasks)):
            pz = pz_next
            if t + 1 < len(tasks):
                pz_next = do_proj(t + 1)
            pi, j = tasks[t]
            if pi < 4:
                c = 2 * pi + j
                op('act', lambda e, c=c, pz=pz: e.activation(out=uT[:, c, :nco], in_=pz, func=AF.Copy),
                   r=[r_pp[j]], w=[r_u[c]])
            else:
                isq = pi < 8
                c = 2 * (pi - 4) + j if isq else j
                gv = vec["qg"] if isq else vec["kg"]
                op('act', lambda e, pz=pz: e.activation(out=sqb[:, :nco], in_=pz, func=AF.Square), r=[r_pp[j]], w=[r_sqb])
                for (c0, n) in st_chunks(s):
                    op('pe', lambda e, c0=c0, n=n: e.matmul(pp[2][:, po + c0:po + c0 + n], lhsT=cbf["blk64"][:],
                                                              rhs=sqb[:, c0:c0 + n], start=True, stop=True),
                       r=[r_sqb, r_c], w=[r_pp[2]])
                op('act', lambda e: e.activation(out=scr[2][:, :nco], in_=pp[2][:, po:1024], func=AF.Sqrt, scale=1.0 / 64,
                                                  bias=EPS_AP), r=[r_pp[2], r_small], w=[r_s[2]])
                op('dve', lambda e: e.reciprocal(out=scr[2][:, :nco], in_=scr[2][:, :nco]), r=[r_s[2]], w=[r_s[2]])
                op('dve', lambda e, pz=pz, gv=gv: e.scalar_tensor_tensor(out=scr[3][:, :nco], in0=pz, scalar=gv[:, 0:1],
                                                                        in1=scr[2][:, :nco], op0=ALU.mult, op1=ALU.mult),
                   r=[r_pp[j], r_s[2], r_c], w=[r_s[3]])
                op('act', lambda e: e.activation(out=sqb[:, :nco], in_=scr[3][:, :nco], func=AF.Copy), r=[r_s[3]], w=[r_sqb])
                for (c0, n) in st_chunks(s):
                    op('pe', lambda e, c0=c0, n=n: e.matmul(pp[2][:, po + c0:po + c0 + n], lhsT=cbf["rotm"][:],
                                                              rhs=sqb[:, c0:c0 + n], start=True, stop=True),
                       r=[r_sqb, r_c], w=[r_pp[2]])
                op(PEL, lambda e: e.tensor_tensor(out=scr[3][:, :nco], in0=scr[3][:, :nco], in1=rcos[:, :nco], op=ALU.mult),
                   r=[r_s[3], r_rope], w=[r_s[3]])
                op('dve', lambda e: e.tensor_tensor(out=scr[2][:, :nco], in0=pp[2][:, po:1024], in1=rsin[:, :nco], op=ALU.mult),
                   r=[r_pp[2], r_rope], w=[r_s[2]])
                if isq:
                    op(PEL, lambda e, c=c: e.tensor_tensor(out=QT[:, c, :nco], in0=scr[3][:, :nco], in1=scr[2][:, :nco],
                                                               op=ALU.add), r=[r_s[3], r_s[2]], w=[r_Q])
                else:
                    op(PEL, lambda e: e.tensor_tensor(out=scr[3][:, :nco], in0=scr[3][:, :nco], in1=scr[2][:, :nco],
                                                          op=ALU.add), r=[r_s[3], r_s[2]], w=[r_s[3]])
                    op('act', lambda e, c=c: e.activation(out=KT[:, c, 128:128 + nco], in_=scr[3][:, :nco], func=AF.Copy),
                       r=[r_s[3]], w=[r_K])
                    if s == 0:
                        op('pe', lambda e: e.transpose(pp[3][0:16, 0:128], scr[3][:, 0:16], cst["ident"][:]),
                           r=[r_s[3], r_c], w=[r_pp[3]])
                        op('dve', lambda e, c=c: e.tensor_copy(out=outst[0:16, 128 * c:128 * (c + 1)], in_=pp[3][0:16, 0:128]),
                           r=[r_pp[3]], w=[r_outst])
                    if s == 3:
                        op('pe', lambda e: e.transpose(pp[3][:, 0:128], scr[3][:, 384:512], cst["ident"][:]),
                           r=[r_s[3], r_c], w=[r_pp[3]])
                        op('dve', lambda e, c=c: e.tensor_copy(out=outst[:, 128 * c:128 * (c + 1)], in_=pp[3][:, 0:128]),
                           r=[r_pp[3]], w=[r_outst])
            if pi == 8 and j == 1:
                if s == 0 and not CFG.get('nocache') and not CFG.get('nok'):
                    P.dma('sp', O["sck"][:, 127, :], outst[0:16, 0:256], r=[r_outst], w=[r_sck])
                    if not CFG.get('nod2d'):
                        P.dma('sp', O["sck"][:, 0:127, :], I["ck"][:, 1:128, :], r=[r_sck], w=[r_sck])
                if s == 3:
                    P.dma('sp', O["pck"][:, :], outst[:, 0:256], r=[r_outst])

        wbuf, wr = mget(s, 9)
        if s == 0:
            bg_precast(2)
        for (slot, c0, n) in tiles:
            pidx = slot % 2
            for k in range(16):
                op('pe', lambda e, k=k, c0=c0, n=n, pidx=pidx: e.matmul(
                    pp[pidx][:n, 0:256], lhsT=hT[:, k, c0:c0 + n], rhs=wbuf[:, k * 256:(k + 1) * 256],
                    start=(k == 0), stop=(k == 15)), r=[wr, r_hT], w=[r_pp[pidx]])
            vs = slot + 1 if s > 0 else slot + 1
            op('act', lambda e, n=n, pidx=pidx, vs=vs: e.activation(
                out=V1[:n, vs, :, 0:64], in_=pp[pidx][:n, 0:256].rearrange("p (a b) -> p a b", a=4), func=AF.Copy),
               r=[r_pp[pidx]], w=[r_V[vs]])
            if s == 0 and slot == 0 and not CFG.get('nocache') and not CFG.get('nov'):
                op('dve', lambda e, pidx=pidx: e.tensor_copy(out=outst[0:16, 256:512], in_=pp[pidx][0:16, 0:256]),
                   r=[r_pp[pidx]], w=[r_outst])
                if not CFG.get('novdma'):
                    P.dma('sp', O["scv"][:, 127, :], outst[0:16, 256:512], r=[r_outst], w=[r_scv])
                if not CFG.get('nod2d'):
                    P.dma('sp', O["scv"][:, 0:127, :], I["cv"][:, 1:128, :], r=[r_scv], w=[r_scv])
            if s == 3 and slot == 3:
                op('dve', lambda e, pidx=pidx: e.tensor_copy(out=outst[:, 256:512], in_=pp[pidx][:, 0:256]),
                   r=[r_pp[pidx]], w=[r_outst])
                P.dma('sp', O["pcv"][:, :], outst[:, 256:512], r=[r_outst])

        tap('uT%d' % s, uT[:, :, :nco], r_u)
        tap('QT%d' % s, QT[:, :, :nco], [r_Q])
        tap('KT%d' % s, KT[:, :, :], [r_K])
        tap('V1%d' % s, V1[:, :, :, :], r_V)
        if CFG['stage'] < 3:
            continue
        npc = nco - pc0

        def ssm_pro(c):
            for (c0, n) in st_chunks(s):
                op('pe', lambda e, c=c, c0=c0, n=n: e.matmul(pp[3][:, po + c0:po + c0 + n], lhsT=Dd[:, c, :], rhs=uT[:, c, c0:c0 + n],
                                                              start=True, stop=False), r=[r_ssmc, r_u[c]], w=[r_pp[3]])
            if s == 0:
                P.dma('sp', h0tok[:, :, 0:64], I["sre"][:, 8 * c:8 * c + 8, :], w=[r_h0])
                P.dma('sp', h0tok[:, :, 64:128], I["sim"][:, 8 * c:8 * c + 8, :], w=[r_h0])
                op('dve', lambda e: e.tensor_scalar(out=h0tok2[:, :, 0:64], in0=h0tok[:, :, 64:128], scalar1=-1.0, scalar2=None,
                                                     op0=ALU.mult), r=[r_h0], w=[r_h0])
                op('dve', lambda e: e.tensor_copy(out=h0tok2[:, :, 64:128], in_=h0tok[:, :, 0:64]), r=[r_h0], w=[r_h0])
                for gi in range(8):
                    op('pe', lambda e, gi=gi: e.transpose(pp[2][:, 16 * gi:16 * gi + 16], h0tok[:, gi, :], cst["ident"][0:16, 0:16]),
                       r=[r_h0, r_c], w=[r_pp[2]])
                    op('pe', lambda e, gi=gi: e.transpose(pp[2][:, 128 + 16 * gi:128 + 16 * gi + 16], h0tok2[:, gi, :],
                                                         cst["ident"][0:16, 0:16]), r=[r_h0, r_c], w=[r_pp[2]])
                op('act', lambda e: e.activation(out=h0T[:], in_=pp[2][:, 0:128].rearrange("p (a b) -> p a b", a=8), func=AF.Copy),
                   r=[r_pp[2]], w=[r_h0])
                op('act', lambda e: e.activation(out=h0T2[:], in_=pp[2][:, 128:256].rearrange("p (a b) -> p a b", a=8), func=AF.Copy),
                   r=[r_pp[2]], w=[r_h0])

        def ssm_tabA(g):
            c, gi = divmod(g, 8)
            g = 8 * c + gi
            j, s_ = gi // 2, gi % 2
            cs = 2 * c + s_
            par = g % 2
            iA, iS, iC, iZ, iY = [5 * par + q for q in range(5)]
            bA, bS, bC, bZ, bY = scr[iA], scr[iS], scr[iC], scr[iZ], scr[iY]
            W1p, W2p, rW1, rW2 = W1x[par], W2x[par], r_W1x[par], r_W2x[par]
            tis = TI[:, pc0:nco]
            thg = T["th"][:, g:g + 1]
            op('act', lambda e: e.activation(out=bA[:, :npc], in_=tis, func=AF.Identity, scale=thg, bias=M_AP),
               r=[r_TI, r_small, r_ssmc], w=[r_s[iA]])
            op('act', lambda e: e.activation(out=bA[:, :npc], in_=bA[:, :npc], func=AF.Identity, bias=NM_AP),
               r=[r_s[iA], r_small], w=[r_s[iA]])
            op('dve', lambda e: e.scalar_tensor_tensor(out=bA[:, :npc], in0=tis, scalar=thg, in1=bA[:, :npc], op0=ALU.mult,
                                                        op1=ALU.subtract), r=[r_TI, r_ssmc, r_s[iA]], w=[r_s[iA]])

        def ssm_tabB(g):
            c, gi = divmod(g, 8)
            g = 8 * c + gi
            j, s_ = gi // 2, gi % 2
            cs = 2 * c + s_
            par = g % 2
            iA, iS, iC, iZ, iY = [5 * par + q for q in range(5)]
            bA, bS, bC, bZ, bY = scr[iA], scr[iS], scr[iC], scr[iZ], scr[iY]
            W1p, W2p, rW1, rW2 = W1x[par], W2x[par], r_W1x[par], r_W2x[par]
            tis = TI[:, pc0:nco]
            thg = T["th"][:, g:g + 1]
            op('act', lambda e: e.activation(out=bS[:, :npc], in_=bA[:, :npc], func=AF.Sin, scale=TWO_PI), r=[r_s[iA]], w=[r_s[iS]])
            op('act', lambda e: e.activation(out=bA[:, :npc], in_=bA[:, :npc], func=AF.Abs), r=[r_s[iA]], w=[r_s[iA]])
            op('act', lambda e: e.activation(out=bC[:, :npc], in_=bA[:, :npc], func=AF.Sin, scale=-TWO_PI, bias=HPI_AP),
               r=[r_s[iA], r_small], w=[r_s[iC]])

        def ssm_xaxb(g):
            c, gi = divmod(g, 8)
            g = 8 * c + gi
            j, s_ = gi // 2, gi % 2
            cs = 2 * c + s_
            par = g % 2
            iA, iS, iC, iZ, iY = [5 * par + q for q in range(5)]
            bA, bS, bC, bZ, bY = scr[iA], scr[iS], scr[iC], scr[iZ], scr[iY]
            W1p, W2p, rW1, rW2 = W1x[par], W2x[par], r_W1x[par], r_W2x[par]
            tis = TI[:, pc0:nco]
            thg = T["th"][:, g:g + 1]
            for v in range(2):
                for (c0, n) in st_chunks(s):
                    op('pe', lambda e, v=v, c0=c0, n=n, j=j, cs=cs, c=c: e.matmul(
                        pp[v][:, po + c0:po + c0 + n], lhsT=Bst[v][32 * j:32 * j + 32, cs, :],
                        rhs=uT[32 * j:32 * j + 32, c, c0:c0 + n], start=True, stop=True, tile_position=(32 * j, 0)),
                       r=[r_ssmc, r_u[c]], w=[r_pp[v]])

        def ssm_main(g):
            c, gi = divmod(g, 8)
            g = 8 * c + gi
            j, s_ = gi // 2, gi % 2
            cs = 2 * c + s_
            par = g % 2
            iA, iS, iC, iZ, iY = [5 * par + q for q in range(5)]
            bA, bS, bC, bZ, bY = scr[iA], scr[iS], scr[iC], scr[iZ], scr[iY]
            W1p, W2p, rW1, rW2 = W1x[par], W2x[par], r_W1x[par], r_W2x[par]
            tis = TI[:, pc0:nco]
            thg = T["th"][:, g:g + 1]
            op('dve', lambda e: e.tensor_tensor(out=bZ[:, :npc], in0=pp[0][:, po + pc0:1024], in1=bC[:, :npc], op=ALU.mult),
               r=[r_pp[0], r_s[iC]], w=[r_s[iZ]])
            op('dve', lambda e: e.tensor_tensor(out=bY[:, :npc], in0=pp[1][:, po + pc0:1024], in1=bS[:, :npc], op=ALU.mult),
               r=[r_pp[1], r_s[iS]], w=[r_s[iY]])
            if s == 0:
                op('dve', lambda e, g=g, gi=gi: e.scalar_tensor_tensor(out=sT[:, gi, :], in0=h0T[:, gi, :], scalar=T["ar"][:, g:g + 1],
                                                                      in1=pp[0][:, po:po + 16], op0=ALU.mult, op1=ALU.add),
                   r=[r_h0, r_ssmc, r_pp[0]], w=[r_h0])
                op('dve', lambda e, g=g, gi=gi: e.scalar_tensor_tensor(out=sT[:, gi, :], in0=h0T2[:, gi, :], scalar=T["ai"][:, g:g + 1],
                                                                      in1=sT[:, gi, :], op0=ALU.mult, op1=ALU.add),
                   r=[r_h0, r_ssmc], w=[r_h0])
            if g + 1 < 64:
                ssm_xaxb(g + 1)
            op('dve', lambda e: e.tensor_tensor(out=bZ[:, :npc], in0=bZ[:, :npc], in1=bY[:, :npc], op=ALU.add),
               r=[r_s[iZ], r_s[iY]], w=[r_s[iZ]])
            op('dve', lambda e, g=g: e.tensor_tensor_scan(out=bY[:, :npc], data0=T["mag"][:, g:g + 1].to_broadcast([128, npc]),
                                                          data1=bZ[:, :npc], initial=T["zl"][:, g:g + 1], op0=ALU.mult, op1=ALU.add),
               r=[r_s[iZ], r_ssmc, r_zl], w=[r_s[iY]])
            op('dve', lambda e, g=g: e.tensor_copy(out=T["zl"][:, g:g + 1], in_=bY[:, npc - 1:npc]),
               r=[r_s[iY]], w=[r_zl])
            op(PEL, lambda e: e.tensor_tensor(out=W1p[:, pc0:nco], in0=bY[:, :npc], in1=bC[:, :npc], op=ALU.mult),
               r=[r_s[iY], r_s[iC]], w=[rW1])
            op(PEL, lambda e: e.tensor_tensor(out=W2p[:, pc0:nco], in0=bY[:, :npc], in1=bS[:, :npc], op=ALU.mult),
               r=[r_s[iY], r_s[iS]], w=[rW2])
            if s == 3:
                op('dve', lambda e, g=g: e.tensor_tensor(out=fin[:, g:g + 1], in0=bY[:, npc - 1:npc], in1=bC[:, npc - 1:npc], op=ALU.mult),
                   r=[r_s[iY], r_s[iC], r_fin], w=[r_fin])
                op('dve', lambda e, g=g: e.tensor_tensor(out=fin[:, 64 + g:65 + g], in0=bY[:, npc - 1:npc], in1=bS[:, npc - 1:npc], op=ALU.mult),
                   r=[r_s[iY], r_s[iS], r_fin], w=[r_fin])
            if s == 0:
                op('dve', lambda e, gi=gi: e.tensor_copy(out=W1p[:, 0:16], in_=sT[:, gi, :]), r=[r_h0], w=[rW1])
            for v, Wx, rW in ((0, W1p, rW1), (1, W2p, rW2)):
                for (c0, n) in st_chunks(s):
                    last = (gi == 7 and v == 1)
                    op('pe', lambda e, v=v, Wx=Wx, c0=c0, n=n, j=j, g=g, last=last: e.matmul(
                        pp[3][32 * j:32 * j + 32, po + c0:po + c0 + n], lhsT=Cst[v][:, g, :], rhs=Wx[:, c0:c0 + n],
                        start=False, stop=last, tile_position=(0, 32 * j)), r=[r_ssmc, rW], w=[r_pp[3]])

        def ssm_epi(c):
            if s == 0:
                for gi in range(8):
                    op('pe', lambda e, gi=gi: e.transpose(pp[2][0:16, 128 * gi:128 * (gi + 1)], sT[:, gi, :], cst["ident"][:]),
                       r=[r_h0, r_c], w=[r_pp[2]])
                op('act', lambda e: e.activation(out=h0tok, in_=pp[2][0:16, 0:1024].rearrange("p (a b) -> p a b", a=8), func=AF.Copy),
                   r=[r_pp[2]], w=[r_h0])
                P.dma('sp', O["ssre"][:, 8 * c:8 * c + 8, :], h0tok[:, :, 0:64], r=[r_h0])
                P.dma('sp', O["ssim"][:, 8 * c:8 * c + 8, :], h0tok[:, :, 64:128], r=[r_h0])
            yv = pp[3][:, po:1024]
            op('act', lambda e, yv=yv: e.activation(out=scr[6][:, :nco], in_=yv, func=AF.Square), r=[r_pp[3]], w=[r_s[6]])
            op('dve', lambda e: e.tensor_scalar(out=scr[6][:, :nco], in0=scr[6][:, :nco], scalar1=0.044715, scalar2=1.0, op0=ALU.mult,
                                                 op1=ALU.add), r=[r_s[6]], w=[r_s[6]])
            op('dve', lambda e, yv=yv: e.tensor_tensor(out=scr[6][:, :nco], in0=yv, in1=scr[6][:, :nco], op=ALU.mult),
               r=[r_pp[3], r_s[6]], w=[r_s[6]])
            op('act', lambda e: e.activation(out=scr[6][:, :nco], in_=scr[6][:, :nco], func=AF.Sigmoid, scale=1.5957691216),
               r=[r_s[6]], w=[r_s[6]])
            op('dve', lambda e, yv=yv, c=c: e.tensor_tensor(out=uT[:, c, :nco], in0=yv, in1=scr[6][:, :nco], op=ALU.mult),
               r=[r_pp[3], r_s[6]], w=[r_u[c]])
        tap('yg%d' % s, uT[:, :, :nco], r_u)
        if CFG['stage'] < 4:
            continue

        if s == 0:
            for par_ in range(2):
                op('dve', lambda e, par_=par_: e.memset(W1x[par_][:, 16:32], 0.0), w=[r_W1x[par_]])
                op('dve', lambda e, par_=par_: e.memset(W2x[par_][:, 0:32], 0.0), w=[r_W2x[par_]])
        ssm_tabA(0)
        ssm_tabA(1)
        ssm_tabB(0)
        ssm_xaxb(0)
        for g in range(64):
            if g + 2 < 64:
                ssm_tabA(g + 2)
            if g + 1 < 64:
                ssm_tabB(g + 1)
            if g % 8 == 0:
                ssm_pro(g // 8)
            if s == 0 and g % 4 == 0:
                bg_precast(3)
            ssm_main(g)
            if g % 8 == 7:
                ssm_epi(g // 8)
        if s == 3:
            op('pe', lambda e: e.transpose(pp[2][0:64, 0:128], fin[:, 0:64], cst["ident"][:]), r=[r_fin, r_c], w=[r_pp[2]])
            op('pe', lambda e: e.transpose(pp[2][0:64, 128:256], fin[:, 64:128], cst["ident"][:]), r=[r_fin, r_c], w=[r_pp[2]])
            op('act', lambda e: e.activation(out=outst[0:64, 256:512], in_=pp[2][0:64, 0:256], func=AF.Copy), r=[r_pp[2]], w=[r_outst])
            op('dve', lambda e: e.tensor_tensor(out=outst[0:64, 0:64], in0=outst[0:64, 256:320], in1=outst[0:64, 448:512], op=ALU.subtract),
               r=[r_outst], w=[r_outst])
            op('dve', lambda e: e.tensor_tensor(out=outst[0:64, 64:128], in0=outst[0:64, 384:448], in1=outst[0:64, 320:384], op=ALU.add),
               r=[r_outst], w=[r_outst])
            P.dma('sp', O["pre"][:, :], outst[0:64, 0:64], r=[r_outst])
            P.dma('sp', O["pim"][:, :], outst[0:64, 64:128], r=[r_outst])
        wg = I["w_glu"].rearrange("(k p) n -> p k n", p=128)
        mixT = hT
        for half in range(2):
            bufs = []
            for q in range(2):
                bufs.append(None)
            for q in range(2):
                wbuf, wr = mget(s, 10 + 2 * half + q)
                for j in range(2):
                    oc = 4 * half + 2 * q + j
                    for (c0, n) in st_chunks(s):
                        for k in range(8):
                            op('pe', lambda e, k=k, c0=c0, n=n, j=j, wbuf=wbuf: e.matmul(
                                pp[j][:, po + c0:po + c0 + n], lhsT=wbuf[:, k * 256 + 128 * j:k * 256 + 128 * (j + 1)],
                                rhs=uT[:, k, c0:c0 + n], start=(k == 0), stop=(k == 7)), r=[wr] + r_u, w=[r_pp[j]])
                    op('act', lambda e, j=j, oc=oc: e.activation(out=scr[2][:, :nco], in_=pp[j][:, po:1024], func=AF.Sigmoid,
                                                                  bias=vec["b_glu"][:, oc:oc + 1]), r=[r_pp[j], r_c], w=[r_s[2]])
                    op('dve', lambda e, oc=oc: e.tensor_tensor(out=scr[2][:, :nco], in0=scr[2][:, :nco], in1=uT[:, oc, :nco], op=ALU.mult),
                       r=[r_s[2]] + r_u, w=[r_s[2]])
                    op('act', lambda e: e.activation(out=sqb[:, :nco], in_=scr[2][:, :nco], func=AF.Square), r=[r_s[2]], w=[r_sqb])
                    for (c0, n) in st_chunks(s):
                        op('pe', lambda e, c0=c0, n=n, oc=oc: e.matmul(pp[2][:, po + c0:po + c0 + n], lhsT=cbf["ones"][:],
                                                                        rhs=sqb[:, c0:c0 + n], start=(oc == 0), stop=(oc == 7)),
                           r=[r_sqb, r_c], w=[r_pp[2]])
                    op('dve', lambda e, oc=oc: e.tensor_scalar(out=mixT[:, oc, :nco], in0=scr[2][:, :nco], scalar1=vec["g_s"][:, oc:oc + 1],
                                                                scalar2=None, op0=ALU.mult), r=[r_s[2], r_c], w=[r_hT])
        op('act', lambda e: e.activation(out=scr[2][:, :nco], in_=pp[2][:, po:1024], func=AF.Sqrt, scale=1.0 / 1024, bias=EPS_AP),
           r=[r_pp[2], r_small], w=[r_s[2]])
        op('dve', lambda e: e.reciprocal(out=scr[2][:, :nco], in_=scr[2][:, :nco]), r=[r_s[2]], w=[r_s[2]])
        for oc in range(8):
            op('dve', lambda e, oc=oc: e.tensor_tensor(out=mixT[:, oc, :nco], in0=mixT[:, oc, :nco], in1=scr[2][:, :nco], op=ALU.mult),
               r=[r_s[2], r_hT], w=[r_hT])

        tap('mixs%d' % s, hT[:, 0:8, :nco], [r_hT])
        if CFG['stage'] < 5:
            continue
        def att_finish(n, c0):
            op('dve', lambda e: e.memset(small[:n, 2:3], 0.0), r=[r_small], w=[r_small])
            op('act', lambda e: e.activation(out=sqb[:n, 0:1024], in_=att[:n].rearrange("p a b -> p (a b)"), func=AF.Square,
                                              accum_out=small[:n, 2:3]), r=[r_att, r_small], w=[r_sqb, r_small])
            op('act', lambda e: e.activation(out=small[:n, 3:4], in_=small[:n, 2:3], func=AF.Sqrt, scale=1.0 / 1024, bias=EPS_AP[:n]),
               r=[r_small], w=[r_small])
            op('dve', lambda e: e.reciprocal(out=small[:n, 3:4], in_=small[:n, 3:4]), r=[r_small], w=[r_small])
            op('dve', lambda e: e.tensor_scalar(out=attn[:n, :], in0=att[:n].rearrange("p a b -> p (a b)"), scalar1=small[:n, 3:4],
                                                 scalar2=None, op0=ALU.mult), r=[r_att, r_small], w=[r_attn])
            pv = pbf(2)
            for jj in range(8):
                op('pe', lambda e, jj=jj: e.transpose(pv[:, 128 * jj:128 * jj + n], attn[:n, 128 * jj:128 * (jj + 1)], cbf["ident"][:n, :n]),
                   r=[r_attn, r_c], w=[r_pp[2]])
            for jj in range(8):
                op('dve', lambda e, jj=jj: e.tensor_scalar(out=mixT[:, 8 + jj, c0:c0 + n], in0=pv[:, 128 * jj:128 * jj + n],
                                                            scalar1=vec["g_a"][:, jj:jj + 1], scalar2=None, op0=ALU.mult),
                   r=[r_pp[2], r_c], w=[r_hT])

        def att_norm_kvh(n, kvh, ovw):
            op('dve', lambda e: e.tensor_tensor(out=small[:n, 8:12], in0=ovw[:, :, 64], in1=esink[:n, 4 * kvh:4 * kvh + 4], op=ALU.add),
               r=[r_pp[3], r_c, r_small], w=[r_small])
            op('dve', lambda e: e.reciprocal(out=small[:n, 8:12], in_=small[:n, 8:12]), r=[r_small], w=[r_small])
            op('dve', lambda e: e.tensor_tensor(out=att[:n, 4 * kvh:4 * kvh + 4, :], in0=ovw[:, :, 0:64],
                                                 in1=small[:n, 8:12].unsqueeze(2).to_broadcast([n, 4, 64]), op=ALU.mult),
               r=[r_pp[3], r_small], w=[r_att])

        for (slot, c0, n) in tiles:
            special = (s == 0 and slot == 0)
            if special:
                prev = None
                mc = cbf["m48c"]
            else:
                mc = cbf["maskc"]
                if s == 0 and slot == 1:
                    prev = (0, 48, cbf["m48p"], 1)
                else:
                    prev = (c0 - 128, 128, cbf["maskp"], slot if s > 0 else slot)
            vcur = slot + 1
            for kvh in range(4):
                pb = 64 * (kvh % 2)
                kc = kvh // 2
                q0 = 4 * (kvh // 2)
                ps_ = pp[kvh % 2]
                rps = r_pp[kvh % 2]
                op('pe', lambda e, ps_=ps_, pb=pb, kc=kc, q0=q0: e.matmul(
                    ps_[:n, 0:4 * n].rearrange("p (a b) -> p a b", a=4), lhsT=KT[pb:pb + 64, kc, 128 + c0:128 + c0 + n],
                    rhs=QT[pb:pb + 64, q0:q0 + 4, c0:c0 + n], start=True, stop=True, tile_position=(pb, 0)),
                   r=[r_K, r_Q], w=[rps])
                op('act', lambda e, ps_=ps_: e.activation(out=PTc[:n, :, :n], in_=ps_[:n, 0:4 * n].rearrange("p (a b) -> p a b", a=4),
                                                          func=AF.Exp, scale=0.125), r=[rps], w=[r_PTc])
                op(PEL, lambda e, mc=mc: e.tensor_tensor(out=PTc[:n, :, :n], in0=PTc[:n, :, :n],
                                                             in1=mc[:n, :n].unsqueeze(1).to_broadcast([n, 4, n]), op=ALU.mult),
                   r=[r_PTc, r_c], w=[r_PTc])
                if prev is not None:
                    pc, nk, mp, vsl = prev
                    op('pe', lambda e, ps_=ps_, pb=pb, kc=kc, q0=q0, pc=pc, nk=nk: e.matmul(
                        ps_[:nk, 512:512 + 4 * n].rearrange("p (a b) -> p a b", a=4), lhsT=KT[pb:pb + 64, kc, 128 + pc:128 + pc + nk],
                        rhs=QT[pb:pb + 64, q0:q0 + 4, c0:c0 + n], start=True, stop=True, tile_position=(pb, 0)),
                       r=[r_K, r_Q], w=[rps])
                    op('act', lambda e, ps_=ps_, nk=nk: e.activation(out=PTp[:nk, :, :n],
                                                                     in_=ps_[:nk, 512:512 + 4 * n].rearrange("p (a b) -> p a b", a=4),
                                                                     func=AF.Exp, scale=0.125), r=[rps], w=[r_PTp])
                    op(PEL, lambda e, mp=mp, nk=nk: e.tensor_tensor(out=PTp[:nk, :, :n], in0=PTp[:nk, :, :n],
                                                                        in1=mp[:nk, :n].unsqueeze(1).to_broadcast([nk, 4, n]), op=ALU.mult),
                       r=[r_PTp, r_c], w=[r_PTp])
                ov = pp[3][:n, 0:260].rearrange("p (a b) -> p a b", a=4)
                for g4 in range(4):
                    if prev is not None:
                        pc, nk, mp, vsl = prev
                        op('pe', lambda e, g4=g4, nk=nk, vsl=vsl, kvh=kvh: e.matmul(pp[3][:n, 65 * g4:65 * g4 + 65], lhsT=PTp[:nk, g4, :n],
                                                                                  rhs=V1[:nk, vsl, kvh, :], start=True, stop=False),
                           r=[r_PTp, r_V[vsl]], w=[r_pp[3]])
                    op('pe', lambda e, g4=g4, kvh=kvh: e.matmul(pp[3][:n, 65 * g4:65 * g4 + 65], lhsT=PTc[:n, g4, :n],
                                                              rhs=V1[:n, vcur, kvh, :], start=(prev is None), stop=True),
                       r=[r_PTc, r_V[vcur]], w=[r_pp[3]])
                att_norm_kvh(n, kvh, ov)
            if special:
                for hb in range(2):
                    P.dma('pool', Kb, O["sck"][8 * hb:8 * hb + 8, :, :].rearrange("b k f -> k b f"), r=[r_sck], w=[r_Kb])
                    for a4 in range(4):
                        P.dma('pool', Vb[:, :, a4, 0:64], O["scv"][8 * hb:8 * hb + 8, :, 64 * a4:64 * a4 + 64].rearrange("b k d -> k b d"),
                              r=[r_scv], w=[r_Vb])
                    for bb in range(8):
                        b = 8 * hb + bb
                        pv = pbf(2)
                        for kc in range(2):
                            op('pe', lambda e, bb=bb, kc=kc: e.transpose(pv[:, 128 * kc:128 * (kc + 1)], Kb[:, bb, 128 * kc:128 * (kc + 1)],
                                                                        cbf["ident"][:]), r=[r_Kb, r_c], w=[r_pp[2]])
                        op('act', lambda e: e.activation(out=KbT, in_=pv[:, 0:256].rearrange("p (a b) -> p a b", a=2), func=AF.Copy),
                           r=[r_pp[2]], w=[r_KbT])
                        for kvh in range(4):
                            pb = 64 * (kvh % 2)
                            kc = kvh // 2
                            q0 = 4 * (kvh // 2)
                            op('pe', lambda e, b=b, pb=pb, kc=kc, q0=q0, kvh=kvh: e.matmul(
                                pp[0][:, 16 * b + 4 * kvh:16 * b + 4 * kvh + 4], lhsT=KbT[pb:pb + 64, kc, :],
                                rhs=QT[pb:pb + 64, q0:q0 + 4, b], start=True, stop=True, tile_position=(pb, 0)),
                               r=[r_KbT, r_Q], w=[r_pp[0]])
                    op('act', lambda e, hb=hb: e.activation(out=PTs[:, 128 * hb:128 * hb + 128], in_=pp[0][:, 128 * hb:128 * hb + 128],
                                                            func=AF.Exp, scale=0.125), r=[r_pp[0]], w=[r_PTs])
                    for bb in range(8):
                        b = 8 * hb + bb
                        for kvh in range(4):
                            op('pe', lambda e, b=b, bb=bb, kvh=kvh: e.matmul(pp[1][0:65, 16 * b + 4 * kvh:16 * b + 4 * kvh + 4],
                                                                           lhsT=Vb[:, bb, kvh, :], rhs=PTs[:, 16 * b + 4 * kvh:16 * b + 4 * kvh + 4],
                                                                           start=True, stop=True), r=[r_Vb, r_PTs], w=[r_pp[1]])
                op('act', lambda e: e.activation(out=OTs, in_=pp[1][0:65, 0:256], func=AF.Copy), r=[r_pp[1]], w=[r_OTs])
                for kvh in range(4):
                    for g4 in range(4):
                        h = 4 * kvh + g4
                        op('pe', lambda e, h=h, g4=g4: e.transpose(pp[3][0:16, 65 * g4:65 * g4 + 65],
                                                                  OTs.rearrange("p (b h) -> p h b", h=16)[:, h, :],
                                                                  cst["ident"][0:65, 0:65]), r=[r_OTs, r_c], w=[r_pp[3]])
                    att_norm_kvh(16, kvh, pp[3][0:16, 0:260].rearrange("p (a b) -> p a b", a=4))
            att_finish(n, c0)
        if s < 3:
            op('act', lambda e: e.activation(out=KT[:, :, 0:128], in_=KT[:, :, nco:nco + 128], func=AF.Copy), r=[r_K], w=[r_K])
            lastv = tiles[-1][0] + 1
            op('act', lambda e: e.activation(out=V1[:, 0 if s > 0 else 0], in_=V1[:, lastv], func=AF.Copy), r=[r_V[lastv]], w=[r_V[0]])

        tap('mixa%d' % s, hT[:, 8:16, :nco], [r_hT])
        if CFG['stage'] < 6:
            continue
        wo = I["w_out"].rearrange("(k p) n -> p k n", p=128)
        for pi in range(8):
            wbuf, wr = mget(s, 14 + pi)
            for (slot, c0, n) in tiles:
                pidx = slot % 2
                for k in range(16):
                    op('pe', lambda e, k=k, c0=c0, n=n, pidx=pidx: e.matmul(pp[pidx][:n, 0:256], lhsT=mixT[:, k, c0:c0 + n],
                                                                          rhs=wbuf[:, k * 256:(k + 1) * 256], start=(k == 0), stop=(k == 15)),
                       r=[wr, r_hT], w=[r_pp[pidx]])
                op('dve', lambda e, n=n, pidx=pidx, slot=slot, pi=pi: e.tensor_tensor(
                    out=xres[:n, slot, 256 * pi:256 * (pi + 1)], in0=pp[pidx][:n, 0:256], in1=xres[:n, slot, 256 * pi:256 * (pi + 1)],
                    op=ALU.add), r=[r_pp[pidx], r_x[slot]], w=[r_x[slot]])

        tap('xmid%d' % s, xres[:, :, :], r_x)
        if CFG['stage'] < 7:
            continue
        norm_to_hT(s, "g_ffn")
        wu = I["w_up"].rearrange("(k p) n -> p k n", p=128)
        wd = I["w_down"].rearrange("(k p) n -> p k n", p=128)
        BV = [(wb[0], r_wb[0]), (wb[1], r_wb[1])]
        BG = [(wb[2], r_wb[2]), (wb[3], r_wb[3])]
        BDx = [(wb[4], r_wb[4]), (wb[5], r_wb[5])]

        def load_up(g):
            wfill(1, BV[g % 2][0], BV[g % 2][1], sc_ffn[3 * g], ('ffn', 3 * g), 4096, None, 16)
            wfill(1, BG[g % 2][0], BG[g % 2][1], sc_ffn[3 * g + 1], ('ffn', 3 * g + 1), 4096, None, 16)

        def load_dn(g):
            wfill(1, BDx[g % 2][0], BDx[g % 2][1], sc_ffn[3 * g + 2], ('ffn', 3 * g + 2), 4096, None, 2)

        def tok_load(g):
            scv_ = I["sconv"].rearrange("b j (v f) -> (b j) v f", v=2)[:, :, 256 * g:256 * (g + 1)]
            P.dma('act', tokst.rearrange("p (v f) -> p v f", v=2), scv_, w=[r_tokst])

        def tok_T():
            for q in range(4):
                op('pe', lambda e, q=q: e.transpose(pp[3][:, 32 * q:32 * q + 32], tokst[:, 128 * q:128 * (q + 1)], cst["ident"][0:32, 0:32]),
                   r=[r_tokst, r_c], w=[r_ppb[3][0]])
            op('act', lambda e: e.activation(out=stT[:], in_=pp[3][:, 0:128].rearrange("p (a b) -> p a b", a=4), func=AF.Copy),
               r=[r_ppb[3][0]], w=[r_stT])

        def ffn_up(g):
            bv, rv = BV[g % 2]
            bg, rg = BG[g % 2]
            actT, r_actT = actTx[g % 2], r_actTx[g % 2]
            if s == 0 and g + 1 < 22:
                tok_load(g + 1)
            for j in range(2):
                c = 2 * g + j
                for vi, (wbuf, wr) in enumerate(((bv, rv), (bg, rg))):
                    cc = c + 44 * vi
                    pz = proj_fm(s, wbuf, wr, j, 256, vi)
                    ub = scr[vi]
                    cvb = scr[2 + vi]
                    rc = r_s[2 + vi]
                    op('act', lambda e, ub=ub, cc=cc: e.activation(out=ub[:, 0:2], in_=convst[:, cc, :], func=AF.Copy),
                       r=[r_convst], w=[r_s[vi]])
                    op('act', lambda e, ub=ub, pz=pz: e.activation(out=ub[:, 2:2 + nco], in_=pz, func=AF.Copy), r=[r_pp[vi]], w=[r_s[vi]])
                    op('act', lambda e, cvb=cvb, pz=pz, cc=cc: e.activation(out=cvb[:, :nco], in_=pz, func=AF.Identity,
                                                                         scale=vec["cw"][:, 2, cc:cc + 1], bias=vec["cb"][:, cc:cc + 1]),
                       r=[r_pp[vi], r_c], w=[rc])
                    op('act', lambda e, ub=ub, cc=cc: e.activation(out=convst[:, cc, :], in_=ub[:, nco:nco + 2], func=AF.Copy),
                       r=[r_s[vi]], w=[r_convst])
                    op('dve', lambda e, ub=ub, cvb=cvb, cc=cc: e.scalar_tensor_tensor(out=cvb[:, :nco], in0=ub[:, 1:1 + nco], scalar=vec["cw"][:, 1, cc:cc + 1],
                                                                                     in1=cvb[:, :nco], op0=ALU.mult, op1=ALU.add),
                       r=[r_s[vi], r_c, rc], w=[rc])
                    op('dve', lambda e, ub=ub, cvb=cvb, cc=cc: e.scalar_tensor_tensor(out=cvb[:, :nco], in0=ub[:, 0:nco], scalar=vec["cw"][:, 0, cc:cc + 1],
                                                                                     in1=cvb[:, :nco], op0=ALU.mult, op1=ALU.add),
                       r=[r_s[vi], r_c, rc], w=[rc])
                    if s == 0:
                        q = 2 * vi + j
                        st3 = stT[:, q, :].rearrange("p (b t) -> p b t", t=2)
                        op('dve', lambda e, ub=ub, cvb=cvb, cc=cc: e.tensor_scalar(out=cvb[:, 0:16], in0=ub[:, 2:18], scalar1=vec["cw"][:, 2, cc:cc + 1],
                                                                                  scalar2=vec["cb"][:, cc:cc + 1], op0=ALU.mult, op1=ALU.add),
                           r=[r_s[vi], r_c, rc], w=[rc])
                        for t in range(2):
                            op('dve', lambda e, cvb=cvb, cc=cc, st3=st3, t=t: e.scalar_tensor_tensor(
                                out=cvb[:, 0:16], in0=st3[:, :, t], scalar=vec["cw"][:, t, cc:cc + 1], in1=cvb[:, 0:16], op0=ALU.mult, op1=ALU.add),
                               r=[r_stT, r_c, rc], w=[rc])
                        op('act', lambda e, ub=ub, q=q: e.activation(out=supb[:, q, :], in_=ub[:, 2:18], func=AF.Copy),
                           r=[r_s[vi]], w=[r_supb])
                    if g >= 1:
                        ffn_down(g - 1, part=2 * j + vi)
                op('act', lambda e: e.activation(out=scr[4][:, :nco], in_=scr[3][:, :nco], func=AF.Silu), r=[r_s[3]], w=[r_s[4]])
                op('pool', lambda e, j=j, actT=actT: e.tensor_tensor(out=actT[:, j, :nco], in0=scr[4][:, :nco], in1=scr[2][:, :nco], op=ALU.mult),
                   r=[r_s[2], r_s[4]], w=[r_actT])
            if s == 0:
                for q in range(4):
                    op('pe', lambda e, q=q: e.transpose(pp[3][0:16, 128 * q:128 * (q + 1)], supb[:, q, :], cst["ident"][:]),
                       r=[r_supb, r_c], w=[r_ppb[3][0]])
                op('act', lambda e: e.activation(out=outst[0:16, :], in_=pp[3][0:16, 0:512], func=AF.Copy), r=[r_ppb[3][0]], w=[r_outst])
                if g + 1 < 22:
                    tok_T()
                if s == 0:
                    P.dma('act', O["ssconv"][:, 1, :].rearrange("b (v f) -> b v f", v=2)[:, :, 256 * g:256 * (g + 1)],
                          outst[0:16, :].rearrange("p (v f) -> p v f", v=2), r=[r_outst])
                else:
                    P.dma('act', O["pconv"].rearrange("b (v f) -> b v f", v=2)[:, :, 256 * g:256 * (g + 1)],
                          outst[0:2, :].rearrange("p (v f) -> p v f", v=2), r=[r_outst])

        def ffn_down(g, part=None):
            bd, rd = BDx[g % 2]
            actT, r_actT = actTx[g % 2], r_actTx[g % 2]
            units = [(slot, c0, n, nn) for (slot, c0, n) in tiles for nn in range(4)]
            if part is not None:
                per = (len(units) + 3) // 4
                units = units[per * part:per * (part + 1)]
            for (slot, c0, n, nn) in units:
                if True:
                    pidx = 2 + (nn % 2)
                    hb = nn // 2
                    for j in range(2):
                        op('pe', lambda e, j=j, c0=c0, n=n, nn=nn, pidx=pidx, hb=hb: e.matmul(
                            pp[pidx][:n, 512 * hb:512 * hb + 512], lhsT=actT[:, j, c0:c0 + n],
                            rhs=bd[:, j * 2048 + 512 * nn:j * 2048 + 512 * (nn + 1)], start=(j == 0), stop=(j == 1)),
                           r=[rd, r_actT], w=[r_ppb[pidx][hb]])
                    op('dve', lambda e, n=n, nn=nn, pidx=pidx, slot=slot, hb=hb: e.tensor_tensor(
                        out=xres[:n, slot, 512 * nn:512 * (nn + 1)], in0=pp[pidx][:n, 512 * hb:512 * hb + 512],
                        in1=xres[:n, slot, 512 * nn:512 * (nn + 1)], op=ALU.add), r=[r_ppb[pidx][hb], r_x[slot]], w=[r_x[slot]])

        if s == 0:
            bg_precast(66, flush=True)
        if s == 0:
            tok_load(0)
            tok_T()
        load_up(0)
        load_dn(0)
        load_up(1)
        load_dn(1)
        for g in range(22):
            ffn_up(g)
            if g + 2 < 22:
                load_up(g + 2)
            if g >= 1 and g + 1 < 22:
                load_dn(g + 1)
        ffn_down(21)
        if s == 3:
            for q8 in range(11):
                for i8 in range(8):
                    cc = 8 * q8 + i8
                    op('pe', lambda e, cc=cc, i8=i8: e.transpose(pp[2][0:2, 128 * i8:128 * (i8 + 1)], convst[:, cc, :], cst["ident"][:]),
                       r=[r_convst, r_c], w=[r_pp[2]])
                op('act', lambda e: e.activation(out=scr[0][0:2, 0:512], in_=pp[2][0:2, 0:512], func=AF.Copy), r=[r_pp[2]], w=[r_s[0]])
                op('act', lambda e: e.activation(out=scr[1][0:2, 0:512], in_=pp[2][0:2, 512:1024], func=AF.Copy), r=[r_pp[2]], w=[r_s[1]])
                P.dma('act', O["pconv"][:, 1024 * q8:1024 * q8 + 512], scr[0][0:2, 0:512], r=[r_s[0]])
                P.dma('act', O["pconv"][:, 1024 * q8 + 512:1024 * (q8 + 1)], scr[1][0:2, 0:512], r=[r_s[1]])
        for (slot, c0, n) in tiles:
            if s == 0 and slot == 0:
                P.dma('sp', O["y_s"][:, :], xres[0:16, 0, :], r=[r_x[0]])
            else:
                gi = y_rows[(s, slot)]
                P.dma('sp', O["y_p"][128 * gi:128 * (gi + 1), :], xres[:, slot, :], r=[r_x[slot]])
    P.dma('sp', O["ssconv"][:, 0, :], I["sconv"][:, 1, :])


_NC_CACHE = {}


def _consts():
    c = {}
    c["ident"] = np.eye(128, dtype=np.float32)
    rot = np.zeros((128, 128), np.float32)
    for m in range(128):
        if (m % 64) < 32:
            rot[m + 32, m] = -1.0
        else:
            rot[m - 32, m] = 1.0
    c["rotm"] = rot
    blk = np.zeros((128, 128), np.float32)
    blk[0:64, 0:64] = 1.0
    blk[64:128, 64:128] = 1.0
    c["blk64"] = blk
    c["ones"] = np.ones((128, 128), np.float32)
    k = np.arange(128)[:, None]
    q = np.arange(128)[None, :]
    c["maskc"] = (k <= q).astype(np.float32)
    c["maskp"] = (k > q).astype(np.float32)
    m48c = np.zeros((128, 128), np.float32)
    m48c[32:48, 32:48] = (k[0:16] <= q[:, 0:16]).astype(np.float32)
    c["m48c"] = m48c
    m48p = np.zeros((128, 128), np.float32)
    m48p[32:48, :] = (k[0:16] > (q - 112)).astype(np.float32)
    c["m48p"] = m48p
    posc = np.zeros((4, 560), np.float32)
    posc[0, 0:16] = 16384.0
    posc[0, 32:560] = np.arange(528)
    for s in range(1, 4):
        posc[s, 0:512] = 16 + 512 * s + np.arange(512)
    c["posc"] = posc
    inv = (10000.0 ** (-np.arange(32, dtype=np.float32) * 2.0 / 64)).astype(np.float32)
    c["ropeinv"] = np.tile(inv, 4).reshape(128, 1).astype(np.float32)
    return c


def prep_inputs(x_prompt, x_sample, cache_k, cache_v, state_ssm_re, state_ssm_im, state_conv,
                meta_tokens, norm_mix_g, w_in, q_norm_g, k_norm_g, attn_sinks, lam_re, lam_im, log_dt,
                ssm_b_re, ssm_b_im, ssm_c_re, ssm_c_im, ssm_d, w_glu, b_glu, ssm_out_g, attn_out_g,
                w_out, norm_ffn_g, w_up, conv_w, conv_b, w_down, cores=range(8)):
    f = lambda a: np.ascontiguousarray(np.asarray(a, dtype=np.float32))
    w_in0 = f(w_in)[0]
    qcols = np.concatenate([np.arange(1024 + 64 * h, 1024 + 64 * (h + 1)) for h in QHEAD_ORDER])
    w_in_p = np.concatenate([w_in0[:, 0:1024], w_in0[:, qcols], w_in0[:, 2048:2560]], axis=1)
    shared = dict(
        meta=f(meta_tokens), g_mix=f(norm_mix_g)[0].reshape(16, 128).T, g_ffn=f(norm_ffn_g)[0].reshape(16, 128).T,
        w_in=w_in_p, qg=np.tile(f(q_norm_g)[0], 2).reshape(128, 1), kg=np.tile(f(k_norm_g)[0], 2).reshape(128, 1),
        sinks=f(attn_sinks), lamre2=np.concatenate([f(lam_re)[0].T] * 2, 0), lamim2=np.concatenate([f(lam_im)[0].T] * 2, 0),
        logdt2=np.broadcast_to(f(log_dt)[0][None, :], (128, 64)),
        bre_t=f(ssm_b_re)[0].transpose(1, 0, 2), bim_t=f(ssm_b_im)[0].transpose(1, 0, 2),
        cre_t=f(ssm_c_re)[0].transpose(2, 0, 1), cim_t=f(ssm_c_im)[0].transpose(2, 0, 1),
        dsk=f(ssm_d)[0].reshape(8, 128).T, w_glu=f(w_glu)[0], b_glu=f(b_glu)[0].reshape(8, 128).T,
        g_s=f(ssm_out_g)[0].reshape(8, 128).T, g_a=f(attn_out_g)[0].reshape(8, 128).T,
        w_out=f(w_out)[0], w_up=f(w_up)[0], cw=f(conv_w)[0].reshape(3, 88, 128).transpose(2, 0, 1),
        cb=f(conv_b)[0].reshape(88, 128).T, w_down=f(w_down)[0],
    )
    shared.update(_consts())
    shared = {k: f(v) for k, v in shared.items()}
    in_maps = []
    for c in cores:
        m = dict(shared)
        sl = slice(16 * c, 16 * c + 16)
        m["xp"] = f(x_prompt[c])
        m["xs"] = f(x_sample[sl, 0])
        m["ck"] = f(cache_k[0, sl]).reshape(16, 128, 256)
        m["cv"] = f(cache_v[0, sl]).reshape(16, 128, 256)
        m["sre"] = f(state_ssm_re[0, sl])
        m["sim"] = f(state_ssm_im[0, sl])
        m["sconv"] = f(state_conv[0, sl])
        in_maps.append(m)
    return in_maps


def kernel(**inputs):
    if "nc" not in _NC_CACHE:
        _NC_CACHE["nc"] = build_nc()
    nc = _NC_CACHE["nc"]
    in_maps = prep_inputs(**inputs)
    res = run_bass_kernel_spmd(nc, in_maps, core_ids=list(range(8)))
    R = res.results
    cat = lambda n: np.concatenate([r[n] for r in R], axis=0)
    stk = lambda n: np.stack([r[n] for r in R], axis=0)
    y_p = stk("y_p")
    y_s = cat("y_s").reshape(128, 1, 2048)
    pck = stk("pck").reshape(1, 8, 128, 4, 64)
    pcv = stk("pcv").reshape(1, 8, 128, 4, 64)
    pre = stk("pre").reshape(1, 8, 64, 64)
    pim = stk("pim").reshape(1, 8, 64, 64)
    pconv = stk("pconv").reshape(1, 8, 2, 11264)
    sck = cat("sck").reshape(1, 128, 128, 4, 64)
    scv = cat("scv").reshape(1, 128, 128, 4, 64)
    ssre = cat("ssre").reshape(1, 128, 64, 64)
    ssim = cat("ssim").reshape(1, 128, 64, 64)
    ssconv = cat("ssconv").reshape(1, 128, 2, 11264)
    return tuple(np.ascontiguousarray(a, dtype=np.float32) for a in
                 (y_p, y_s, pck, pcv, pre, pim, pconv, sck, scv, ssre, ssim, ssconv))
```
